# Optimizing a Trainium2 kernel written in Bass

```python
import jax, jax.numpy as jnp
from jax import lax
import numpy as np

D_MODEL = 1024
BATCH = 8
SEQ = 8192
DEPTH = 1
DEC_BATCH = 128
DEC_SEQ = 4
PAST_LEN = 8192
PAGE_SIZE = 128

HEAD_DIM = 64
N_HEADS = D_MODEL // HEAD_DIM
CONV_HEADS = N_HEADS // 4
CONV_DIM = CONV_HEADS * HEAD_DIM
ATTN_HEADS = N_HEADS - CONV_HEADS
ATTN_DIM = ATTN_HEADS * HEAD_DIM
DIL_PAIRS = ((128, 1), (512, 4), (2048, 16))
N_DIL = len(DIL_PAIRS)
GROUP_HEADS = ATTN_HEADS // N_DIL
CONV_WIDTH = 3
N_BUCKETS = 32
MAX_DISTANCE = 2048
D_FF = ((8 * D_MODEL + 3 * 256 - 1) // (3 * 256)) * 256
PROJ_DIM = 3 * CONV_DIM + 3 * ATTN_DIM
EPS = 1e-6
NEG_INF = -1e30
ATTN_SCALE = HEAD_DIM ** -0.5

kernel_name = "hymba_conv_dilated_swa_adaln_step"


def _rmsnorm(x, g):
    xf = x.astype(jnp.float32)
    y = xf * lax.rsqrt(jnp.mean(xf * xf, axis=-1, keepdims=True) + EPS)
    return (y * g.astype(jnp.float32)).astype(x.dtype)


def _t5_buckets(dist):
    dist = np.asarray(dist, np.int32)
    max_exact = N_BUCKETS // 2
    large = max_exact + (np.log(np.maximum(dist, 1).astype(np.float32) / max_exact)
                         / np.log(MAX_DISTANCE / max_exact) * (N_BUCKETS - max_exact)).astype(np.int32)
    large = np.minimum(large, N_BUCKETS - 1)
    return np.where(dist < max_exact, dist, large).astype(np.int32)


def _group_bias(rel_bias, g):
    w, d = DIL_PAIRS[g]
    nk = w // d + 1
    buckets = _t5_buckets(np.arange(nk) * d)
    hs = slice(g * GROUP_HEADS, (g + 1) * GROUP_HEADS)
    return rel_bias[buckets][:, hs].T.astype(jnp.float32)


def _dilated_prompt(q, k, v, bias, d):
    Bn, S, H, Dh = q.shape
    qb = bias.shape[1] - 1
    M = S // d
    nb = -(-M // qb)
    Mp = nb * qb

    def to_res(t):
        t = t.reshape(Bn, M, d, H, Dh).transpose(0, 2, 1, 3, 4)
        return jnp.pad(t, ((0, 0), (0, 0), (0, Mp - M), (0, 0), (0, 0)))

    qr = to_res(q).reshape(Bn, d, nb, qb, H, Dh)
    pad_front = ((0, 0), (0, 0), (qb, 0), (0, 0), (0, 0))
    kp = jnp.pad(to_res(k), pad_front)
    vp = jnp.pad(to_res(v), pad_front)

    def blocks(t):
        prev = t[:, :, :Mp].reshape(Bn, d, nb, qb, H, Dh)
        cur = t[:, :, qb:].reshape(Bn, d, nb, qb, H, Dh)
        return jnp.concatenate([prev, cur], axis=3)

    kb, vb = blocks(kp), blocks(vp)
    qq = np.arange(qb)[:, None]
    kk = np.arange(2 * qb)[None, :]
    steps = qb + qq - kk
    band = (steps >= 0) & (steps <= qb)
    first_ok = kk >= qb
    mask = band[None] & ((np.arange(nb)[:, None, None] > 0) | first_ok[None])
    bias_band = bias[:, np.clip(steps, 0, qb)]

    s = jnp.einsum('bdnqhe,bdnkhe->bdnhqk', qr, kb).astype(jnp.float32) * ATTN_SCALE
    s = s + bias_band[None, None, None]
    s = jnp.where(jnp.asarray(mask)[None, None, :, None], s, NEG_INF)
    m = jnp.max(s, axis=-1, keepdims=True)
    e = jnp.exp(s - m)
    den = jnp.sum(e, axis=-1, keepdims=True)
    lse = (m + jnp.log(den))[..., 0]
    o = jnp.einsum('bdnhqk,bdnkhe->bdnqhe', (e / den).astype(v.dtype), vb)
    o = o.reshape(Bn, d, Mp, H, Dh)[:, :, :M].transpose(0, 2, 1, 3, 4).reshape(Bn, S, H, Dh)
    lse = lse.transpose(0, 1, 2, 4, 3).reshape(Bn, d, Mp, H)[:, :, :M]
    lse = lse.transpose(0, 2, 1, 3).reshape(Bn, S, H)
    return o, lse


def _dilated_sample(q, k, v, kv_buf, bias, d):
    Bn, T, H, Dh = q.shape
    L = kv_buf.shape[1]
    nk = bias.shape[1]
    k_ext = jnp.concatenate([kv_buf[:, :, 0].astype(k.dtype), k], axis=1)
    v_ext = jnp.concatenate([kv_buf[:, :, 1].astype(v.dtype), v], axis=1)
    idx = L + np.arange(T)[:, None] - np.arange(nk)[None, :] * d
    valid = jnp.asarray(idx >= 0)
    idx_c = np.maximum(idx, 0)
    kg = k_ext[:, idx_c]
    vg = v_ext[:, idx_c]
    s = jnp.einsum('bthe,btjhe->bhtj', q, kg).astype(jnp.float32) * ATTN_SCALE
    s = s + bias[None, :, None, :]
    s = jnp.where(valid[None, None], s, NEG_INF)
    m = jnp.max(s, axis=-1, keepdims=True)
    e = jnp.exp(s - m)
    den = jnp.sum(e, axis=-1, keepdims=True)
    lse = (m + jnp.log(den))[..., 0]
    o = jnp.einsum('bhtj,btjhe->bthe', (e / den).astype(v.dtype), vg)
    new_buf = jnp.stack([k_ext[:, T:], v_ext[:, T:]], axis=2)
    return o, lse.transpose(0, 2, 1), new_buf


def _mixer(h, w_in, conv_w, gn_conv, gn_attn, w_out, rel_bias, conv_prev, kv_bufs):
    Bn, T, _ = h.shape
    proj = h @ w_in
    cuts = [CONV_DIM, 2 * CONV_DIM, 3 * CONV_DIM, 3 * CONV_DIM + ATTN_DIM, 3 * CONV_DIM + 2 * ATTN_DIM]
    gb, gc, xin, q, k, v = jnp.split(proj, cuts, axis=-1)

    u = gc * xin
    if conv_prev is None:
        conv_prev = jnp.zeros((Bn, CONV_WIDTH - 1, CONV_DIM), u.dtype)
    u_ext = jnp.concatenate([conv_prev.astype(u.dtype), u], axis=1)
    z = conv_w[0] * u_ext[:, 0:T] + conv_w[1] * u_ext[:, 1:T + 1] + conv_w[2] * u_ext[:, 2:T + 2]
    conv_out = gb * z
    new_conv = u_ext[:, T:]

    q = q.reshape(Bn, T, ATTN_HEADS, HEAD_DIM)
    k = k.reshape(Bn, T, ATTN_HEADS, HEAD_DIM)
    v = v.reshape(Bn, T, ATTN_HEADS, HEAD_DIM)
    outs, lses, new_kv = [], [], []
    for g in range(N_DIL):
        w, d = DIL_PAIRS[g]
        hs = slice(g * GROUP_HEADS, (g + 1) * GROUP_HEADS)
        bias = _group_bias(rel_bias, g)
        qg, kg, vg = q[:, :, hs], k[:, :, hs], v[:, :, hs]
        if kv_bufs is None:
            o, l = _dilated_prompt(qg, kg, vg, bias, d)
            L = min(w, T)
            nb_ = jnp.stack([kg[:, T - L:], vg[:, T - L:]], axis=2)
        else:
            o, l, nb_ = _dilated_sample(qg, kg, vg, kv_bufs[g], bias, d)
        outs.append(o)
        lses.append(l)
        new_kv.append(nb_)
    alpha = jax.nn.softmax(jnp.stack(lses, axis=0), axis=0)
    attn_out = jnp.concatenate(
        [(alpha[g][..., None] * outs[g].astype(jnp.float32)).astype(h.dtype) for g in range(N_DIL)],
        axis=2).reshape(Bn, T, ATTN_DIM)

    mixed = jnp.concatenate([_rmsnorm(conv_out, gn_conv), _rmsnorm(attn_out, gn_attn)], axis=-1)
    return mixed @ w_out, new_conv, new_kv


def _layer(x, c, w_ada, b_ada, norm1_g, norm2_g, w_in, conv_w, gn_conv, gn_attn, w_out,
           w_gate, w_up, w_down, rel_bias, conv_prev, kv_bufs):
    mod = jax.nn.silu(c) @ w_ada + b_ada
    sh1, sc1, ga1, sh2, sc2, ga2 = jnp.split(mod[:, None, :], 6, axis=-1)
    h = _rmsnorm(x, norm1_g) * (1 + sc1) + sh1
    mix, new_conv, new_kv = _mixer(h, w_in, conv_w, gn_conv, gn_attn, w_out, rel_bias, conv_prev, kv_bufs)
    x = x + ga1 * mix
    h2 = _rmsnorm(x, norm2_g) * (1 + sc2) + sh2
    ffn = (jax.nn.silu(h2 @ w_gate) * (h2 @ w_up)) @ w_down
    x = x + ga2 * ffn
    return x, new_conv, new_kv


def setup_inputs(seed: int = 0) -> dict:
    key = jax.random.key(seed)
    ks = jax.random.split(key, 24)
    f32 = jnp.float32
    nrm = lambda k, shape, s=1.0: (jax.random.normal(k, shape, f32) * s)
    buf_lens = [min(w, PAST_LEN) for (w, _) in DIL_PAIRS]
    return {
        "x_prompt": nrm(ks[0], (BATCH, SEQ, D_MODEL)),
        "x_sample": nrm(ks[1], (DEC_BATCH, DEC_SEQ, D_MODEL)),
        "state_conv": nrm(ks[2], (DEPTH, DEC_BATCH, CONV_WIDTH - 1, CONV_DIM)),
        "cache_kv1": nrm(ks[3], (DEPTH, DEC_BATCH, buf_lens[0], 2, GROUP_HEADS, HEAD_DIM)),
        "cache_kv2": nrm(ks[4], (DEPTH, DEC_BATCH, buf_lens[1], 2, GROUP_HEADS, HEAD_DIM)),
        "cache_kv3": nrm(ks[5], (DEPTH, DEC_BATCH, buf_lens[2], 2, GROUP_HEADS, HEAD_DIM)),
        "c_prompt": nrm(ks[6], (BATCH, D_MODEL)),
        "c_sample": nrm(ks[7], (DEC_BATCH, D_MODEL)),
        "w_ada": nrm(ks[8], (DEPTH, D_MODEL, 6 * D_MODEL), 0.5 * D_MODEL ** -0.5),
        "b_ada": nrm(ks[9], (DEPTH, 6 * D_MODEL), 0.01),
        "norm1_g": 1.0 + nrm(ks[10], (DEPTH, D_MODEL), 0.01),
        "norm2_g": 1.0 + nrm(ks[11], (DEPTH, D_MODEL), 0.01),
        "w_in": nrm(ks[12], (DEPTH, D_MODEL, PROJ_DIM), D_MODEL ** -0.5),
        "conv_w": nrm(ks[13], (DEPTH, CONV_WIDTH, CONV_DIM), CONV_WIDTH ** -0.5),
        "gn_conv": 1.0 + nrm(ks[14], (DEPTH, CONV_DIM), 0.01),
        "gn_attn": 1.0 + nrm(ks[15], (DEPTH, ATTN_DIM), 0.01),
        "w_out": nrm(ks[16], (DEPTH, D_MODEL, D_MODEL), D_MODEL ** -0.5),
        "w_gate": nrm(ks[17], (DEPTH, D_MODEL, D_FF), D_MODEL ** -0.5),
        "w_up": nrm(ks[18], (DEPTH, D_MODEL, D_FF), D_MODEL ** -0.5),
        "w_down": nrm(ks[19], (DEPTH, D_FF, D_MODEL), D_FF ** -0.5),
        "rel_bias": nrm(ks[20], (N_BUCKETS, ATTN_HEADS), 0.5),
        "final_g": 1.0 + nrm(ks[21], (D_MODEL,), 0.01),
    }


def reference(x_prompt, x_sample, state_conv, cache_kv1, cache_kv2, cache_kv3, c_prompt, c_sample,
              w_ada, b_ada, norm1_g, norm2_g, w_in, conv_w, gn_conv, gn_attn, w_out,
              w_gate, w_up, w_down, rel_bias, final_g):
    xp, xs = x_prompt, x_sample
    conv_p, kv1_p, kv2_p, kv3_p = [], [], [], []
    conv_s, kv1_s, kv2_s, kv3_s = [], [], [], []
    for l in range(DEPTH):
        xp, cp, kvp = _layer(xp, c_prompt, w_ada[l], b_ada[l], norm1_g[l], norm2_g[l], w_in[l], conv_w[l],
                             gn_conv[l], gn_attn[l], w_out[l], w_gate[l], w_up[l], w_down[l], rel_bias,
                             None, None)
        xs, cs, kvs = _layer(xs, c_sample, w_ada[l], b_ada[l], norm1_g[l], norm2_g[l], w_in[l], conv_w[l],
                             gn_conv[l], gn_attn[l], w_out[l], w_gate[l], w_up[l], w_down[l], rel_bias,
                             state_conv[l], (cache_kv1[l], cache_kv2[l], cache_kv3[l]))
        conv_p.append(cp); kv1_p.append(kvp[0]); kv2_p.append(kvp[1]); kv3_p.append(kvp[2])
        conv_s.append(cs); kv1_s.append(kvs[0]); kv2_s.append(kvs[1]); kv3_s.append(kvs[2])
    y_prompt = _rmsnorm(xp, final_g)
    y_sample = _rmsnorm(xs, final_g)
    return (y_prompt, y_sample,
            jnp.stack(conv_p), jnp.stack(kv1_p), jnp.stack(kv2_p), jnp.stack(kv3_p),
            jnp.stack(conv_s), jnp.stack(kv1_s), jnp.stack(kv2_s), jnp.stack(kv3_s))
```

```python
import numpy as np
from contextlib import ExitStack
import concourse.bass as bass
import concourse.mybir as mybir
from concourse.bass_utils import run_bass_kernel_spmd

F32 = mybir.dt.float32
BF16 = mybir.dt.bfloat16
AF = mybir.ActivationFunctionType
ALU = mybir.AluOpType

NCORES = 8
D = 1024
S = 8192
T = 1024
NST = S // T
DFF = 2816
NFF = DFF // 128
EPS = 1e-6
NEG = -30000.0
SG = (1, 4, 8)
HIST = (128, 128, 256)
CUR = (1024, 256, 128)
NBLK = (2, 2, 3)
LG = (128, 512, 2048)
DIL = (1, 4, 16)
FFPARTS = ((0, 6), (6, 6), (12, 6), (18, 4))


def _t5_buckets(dist):
    dist = np.asarray(dist, np.int32)
    max_exact = 16
    large = max_exact + (np.log(np.maximum(dist, 1).astype(np.float32) / max_exact)
                         / np.log(2048 / max_exact) * (32 - max_exact)).astype(np.int32)
    large = np.minimum(large, 31)
    return np.where(dist < max_exact, dist, large).astype(np.int32)


def _bias_index_tiles():
    out = []
    k = np.arange(128)[:, None, None]
    for g in range(3):
        nb = NBLK[g]
        o = np.arange(nb)[None, :, None]
        i = np.arange(128)[None, None, :]
        delta = i + 128 * o - k
        if g < 2:
            valid = (delta >= 0) & (delta <= 128)
            j = np.clip(delta, 0, 128)
        else:
            valid = (delta >= 0) & (delta <= 256) & (delta % 2 == 0)
            j = np.clip(delta // 2, 0, 128)
        bidx = _t5_buckets(j * DIL[g])
        out.append(np.where(valid, bidx, 32).astype(np.int64))
    return out


ALLBUFS = []


class Buf:
    def __init__(self, name, ranges=()):
        self.name = name
        self.w = None
        self.r = {}
        self.ranges = list(ranges)
        self.ov = None
        self.is_psum = False
        ALLBUFS.append(self)

    def group(self):
        if self.ov is None:
            self.ov = [self]
            for o in ALLBUFS:
                if o is self:
                    continue
                if any(a < d and c < b for (a, b) in self.ranges for (c, d) in o.ranges):
                    self.ov.append(o)
        return self.ov


class Eng:
    def __init__(self, name, is_pe=False):
        self.name = name
        self.prog = []
        self.sem = None
        self.n = 0
        self.seen = {}
        self.is_pe = is_pe
        self.dsems = []
        self.dcnt = []
        self.dn = 0

    def wait(self, ev):
        if ev is None:
            return
        sem, val = ev
        if self.is_pe and sem is self.sem:
            return
        key = id(sem)
        if self.seen.get(key, 0) >= val:
            return
        self.seen[key] = val
        self.prog.append(("w", sem, val))

    def _deps(self, R, W):
        for b0 in R:
            for b in b0.group():
                self.wait(b.w)
            if b0.is_psum:
                for ev in list(b0.r.values()):
                    if ev[0] is not self.sem:
                        self.wait(ev)
        for b0 in W:
            for b in b0.group():
                self.wait(b.w)
                for ev in list(b.r.values()):
                    self.wait(ev)

    def _upd(self, R, W, ev):
        for b in R:
            b.r[id(ev[0])] = ev
        for b in W:
            b.w = ev
            b.r = {}

    def op(self, fn, R=(), W=()):
        self._deps(R, W)
        self.n += 1
        ev = (self.sem, self.n)
        self.prog.append(("i", [fn], self.sem, 1))
        self._upd(R, W, ev)

    def group(self, fns, R=(), W=()):
        self._deps(R, W)
        self.n += 1
        ev = (self.sem, self.n)
        self.prog.append(("i", list(fns), self.sem, 1))
        self._upd(R, W, ev)

    def dma(self, out, in_, R=(), W=(), **kw):
        i = self.dn % len(self.dsems)
        self.dn += 1
        sem = self.dsems[i]
        self.wait((sem, 16 * self.dcnt[i]))
        self._deps(R, W)
        self.dcnt[i] += 1
        ev = (sem, 16 * self.dcnt[i])
        self.prog.append(("i", [lambda e, o=out, a=in_, k=kw: e.dma_start(out=o, in_=a, **k)], sem, 16))
        self._upd(R, W, ev)
        return ev

    def replay(self, e):
        for it in self.prog:
            if it[0] == "w":
                e.wait_ge(it[1], it[2])
            else:
                ins = None
                for fn in it[1]:
                    ins = fn(e)
                ins.then_inc(it[2], it[3])


def build_program(nst=NST, do_sample=True, st_list=None):
    nc = bass.Bass("TRN2", target_bir_lowering=False)
    es = ExitStack()

    def din(name, shape):
        return nc.dram_tensor(name, list(shape), F32, kind="ExternalInput").ap()

    def dout(name, shape):
        return nc.dram_tensor(name, list(shape), F32, kind="ExternalOutput").ap()

    xp = din("xp", [S, D]); xs = din("xs", [64, D]); cpp = din("cc", [17, D])
    sconv = din("sconv", [16, 2, 256])
    kvin = [din("kv1", [16, 128, 512]), din("kv2", [16, 512, 512]), din("kv3", [16, 2048, 512])]
    w_ada = din("w_ada", [D, 6 * D]); b_adaT = din("b_adaT", [128, 48]); bga = din("bga", [17, 2048])
    n1gT = din("n1gT", [128, 8]); n2gT = din("n2gT", [128, 8])
    w_in = din("w_in", [D, 3072]); convwT = din("convwT", [128, 2, 3]); gncT = din("gncT", [128, 2]); gnaT = din("gnaT", [128, 6])
    w_out = din("w_out", [D, D]); w_gate = din("w_gate", [D, DFF]); w_up = din("w_up", [D, DFF]); w_down = din("w_down", [DFF, D])
    fgb = din("fgb", [128, D]); bt01 = din("bt01", [128, 8, 256]); bt2 = din("bt2", [128, 4, 384]); identd = din("ident", [128, 128])
    yp = dout("yp", [S, D]); ys = dout("ys", [64, D]); convp = dout("convp", [2, 256])
    kvp = [dout("kv1p", [128, 512]), dout("kv2p", [512, 512]), dout("kv3p", [2048, 512])]
    convs = dout("convs", [16, 2, 256])
    kvs = [dout("kv1s", [16, 128, 512]), dout("kv2s", [16, 512, 512]), dout("kv3s", [16, 2048, 512])]
    mod_scr = nc.dram_tensor("mod_scr", [17, 2048], F32, kind="Internal").ap()
    kv_scr = nc.dram_tensor("kv_scr", [64, 1536], F32, kind="Internal").ap()
    b_modscr = Buf("modscr"); b_kvscr = Buf("kvscr")
    wsc = {}
    b_wsc = {}
    for nm, shp in (("w_in", [D, 3072]), ("w_out", [D, D]), ("w_gate", [D, DFF]), ("w_up", [D, DFF]), ("w_down", [DFF, D])):
        wsc[nm] = nc.dram_tensor(nm + "_bf", shp, BF16, kind="Internal").ap()
        b_wsc[nm] = Buf(nm + "_bf")
    WSRC = {"w_in": w_in, "w_out": w_out, "w_gate": w_gate, "w_up": w_up, "w_down": w_down}

    arena = {"off": 16512}
    RNG = {}
    del ALLBUFS[:]

    def sb(name, shape, dt, at=None):
        nbytes = int(np.prod(shape[1:])) * (4 if dt == F32 else 2)
        nbytes = (nbytes + 31) // 32 * 32
        if at is None:
            at = arena["off"]
            arena["off"] += nbytes
            assert arena["off"] <= 229376, (name, arena["off"])
        RNG[name] = (at, at + nbytes)
        return nc.alloc_sbuf_tensor_at(name, list(shape), dt, offset=at)

    def region(nbytes):
        at = arena["off"]
        arena["off"] += nbytes
        assert arena["off"] <= 229376, arena["off"]
        return at

    K = 1024
    ident_f = sb("ident_f", [128, 128], F32); ident_b = sb("ident_b", [128, 128], BF16); ones_b = sb("ones_b", [128, 128], BF16)
    expb = [sb("expb01", [128, 8, 256], BF16), sb("expb2", [128, 4, 384], BF16)]
    bts = [sb("bts0", [128, 2, 2, 4, 2], F32), sb("bts1", [128, 4, 2, 2, 2], F32), sb("bts2", [128, 4, 2, 3, 2], F32)]
    Qbd = sb("Qbd", [128, 6, 64, 2], BF16)
    Es2 = [sb("Es2_0", [128, 64], BF16), sb("Es2_1", [128, 64], BF16)]
    Vn0b = sb("Vn0b", [4, 256], BF16); Vn12b = sb("Vn12b", [1, 4, 2, 256], BF16)
    vecs = sb("vecs", [128, 96], F32)
    modT = sb("modT", [128, 32, 17], F32)
    a1T = sb("a1T", [128, 8, 17], F32); a2T = sb("a2T", [128, 8, 17], F32)
    g1p = sb("g1p", [128, D], F32); g2p = sb("g2p", [128, D], F32); fg = sb("fg", [128, D], F32)
    small = sb("small", [128, 64], F32)
    convn = sb("convn", [128, 2, T], BF16)
    stage = [sb("stage0", [128, 512], F32), sb("stage1", [128, 512], F32)]
    xin = [sb("xin0", [128, D], F32), sb("xin1", [128, D], F32)]
    wbuf = [sb("wbuf0", [128, 8, 512], BF16), sb("wbuf1", [128, 8, 512], BF16)]
    hT = sb("hT", [128, 8, T], BF16)
    kv_at = region(60 * K)
    o = kv_at
    KT = []; QT = []; VP = []
    for g in range(3):
        KT.append(sb(f"KT{g}", [128, 2, SG[g], HIST[g] + CUR[g]], BF16, at=o)); o += 2 * SG[g] * (HIST[g] + CUR[g]) * 2
    for g in range(3):
        nb = (HIST[g] + CUR[g]) // 128
        VP.append(sb(f"VP{g}", [128, SG[g], nb, 256], BF16, at=o)); o += SG[g] * nb * 512
    QTall = sb("QTall", [128, 6, T], BF16, at=o)
    for g in range(3):
        QT.append(sb(f"QT{g}", [128, 2, SG[g], CUR[g]], BF16, at=o + g * 2 * T * 2))
    o += 6 * T * 2
    assert o <= kv_at + 60 * K, o - kv_at
    o = kv_at
    A1s = sb("A1s", [128, 8, 64], F32, at=o); o += 2048
    SH1s = sb("SH1s", [128, 8, 64], F32, at=o); o += 2048
    A2s = sb("A2s", [128, 8, 64], F32, at=o); o += 2048
    SH2s = sb("SH2s", [128, 8, 64], F32, at=o); o += 2048
    mixs = sb("mixs", [128, 6, 64], BF16, at=o); o += 768
    blkK = [sb("blkK0", [128, 13, 256], F32, at=o), sb("blkK1", [128, 13, 256], F32, at=o + 13 * 1024)]; o += 13 * 2048
    KTs2 = [sb("KTs0", [128, 13, 2, 128], BF16, at=o), None]; o += 13 * 512
    Vs2 = [sb("Vs0", [128, 13, 256], BF16, at=o), None]; o += 13 * 512
    QTs = sb("QTs", [128, 6, 64], BF16, at=o); o += 768
    KTn = sb("KTn", [128, 6, 64], BF16, at=o); o += 768
    g1s = sb("g1s", [128, D], F32, at=o); o += 4096
    g2s = sb("g2s", [128, D], F32, at=o); o += 4096
    assert o <= kv_at + 60 * K, o - kv_at
    r2 = region(32 * K)
    AT = sb("AT", [128, 6, T], F32, at=r2); Zb = sb("Zb", [128, 2, T], F32, at=r2 + 24 * K)
    x1 = sb("x1", [128, 8, D], F32, at=r2)
    r4 = region(36 * K)
    xnb = [sb("xnb0", [128, D], BF16, at=r4), sb("xnb1", [128, D], BF16, at=r4 + 2 * K)]
    junk = sb("junk", [128, D], BF16, at=r4 + 4 * K)
    GB = sb("GB", [128, 2, 512], F32, at=r4 + 6 * K); GC = sb("GC", [128, 2, 512], F32, at=r4 + 10 * K)
    U = sb("U", [128, 2, 520], F32, at=r4 + 14 * K)
    Zt = sb("Zt", [128, 2, 2, 512], F32, at=r4 + 19 * K)
    Ucar = sb("Ucar", [128, 2, 2], F32)
    Ebuf = [sb("E0", [128, 384], BF16, at=r4), sb("E1", [128, 384], BF16, at=r4 + K), sb("E2", [128, 384], BF16, at=r4 + 2 * K)]
    sq = sb("sq", [128, 6, 512], BF16, at=r4 + 27 * K)
    rsb = sb("rsb", [128, 512], F32, at=r4 + 33 * K)
    Es = sb("Es", [128, 64], BF16, at=r4 + 35 * K)
    Vn0 = sb("Vn0", [4, 256], BF16, at=r4 + 35 * K + 256)
    Vn12 = sb("Vn12", [1, 4, 2, 256], BF16, at=r4 + 6 * K)
    actT = sb("actT", [128, 6, T], BF16, at=r4); wd = sb("wd", [128, 6, D], BF16, at=r4 + 12 * K)
    sgt = [sb("sg0", [128, 512], F32, at=r4 + 24 * K), sb("sg1", [128, 512], F32, at=r4 + 26 * K)]
    KTs2[1] = sb("KTs1", [128, 13, 2, 128], BF16, at=r4 + 10 * K)
    Vs2[1] = sb("Vs1", [128, 13, 256], BF16, at=r4 + 10 * K + 13 * 512)
    pbank = [nc.alloc_psum_tensor(f"pb{i}", [128, 512], F32) for i in range(8)]
    pbuf = [Buf(f"pb{i}") for i in range(8)]
    for b_ in pbuf:
        b_.is_psum = True

    pe = Eng("pe", is_pe=True); act = Eng("act"); dve = Eng("dve"); pool = Eng("pool"); sp = Eng("sp")
    engs = [pe, act, dve, pool, sp]
    print("arena end", arena["off"], 229376 - arena["off"])
    for e in engs:
        e.sem = es.enter_context(nc.semaphore("s_" + e.name))
    for e in (sp, pool):
        for i in range(8):
            e.dsems.append(es.enter_context(nc.semaphore(f"d_{e.name}{i}")))
            e.dcnt.append(0)

    btf01 = sb("btf01", [128, 8, 256], F32, at=r2); btf2 = sb("btf2", [128, 4, 384], F32, at=r2 + 8 * K); bttmp = sb("bttmp", [128, 2048], F32, at=r2 + 16 * K)
    cin = sb("cin", [17, D], F32, at=r4); csil = sb("csil", [17, D], F32, at=r4 + 4 * K); siluT = sb("siluT", [128, 8, 17], BF16, at=r4 + 8 * K)
    gat = sb("gat", [17, 2048], F32, at=r4 + 10 * K); bgat = sb("bgat", [17, 2048], F32, at=r4 + 18 * K)
    GROUPS = {
        "const": ["ident_f", "ident_b", "ones_b", "expb01", "expb2", "bts0", "bts1", "bts2"],
        "vecs": ["vecs"], "modT": ["modT"], "aT": ["a1T", "a2T"], "gp": ["g1p", "g2p", "fg"], "small": ["small"], "convn": ["convn"],
        "hT": ["hT"], "KT": ["KT0", "KT1", "KT2"], "VP": ["VP0", "VP1", "VP2"], "QT": ["QTall"], "AT": ["AT"], "Zb": ["Zb"],
        "junk": ["junk"], "GB": ["GB"], "GC": ["GC"], "U": ["U"], "Zt": ["Zt"], "Ucar": ["Ucar"], "sq": ["sq"], "rsb": ["rsb"],
        "actT": ["actT"], "wd": ["wd"], "samp": ["A1s", "SH1s", "A2s", "SH2s", "g1s", "g2s"], "blk0": ["blkK0"], "blk1": ["blkK1"], "KTs0": ["KTs0"], "KTs1": ["KTs1"], "Vs0": ["Vs0"], "Vs1": ["Vs1"],
        "Es": ["Es"], "Qbd": ["Qbd"], "Es2_0": ["Es2_0"], "Es2_1": ["Es2_1"], "Vnb0": ["Vn0", "Vn12"], "Vnb1": ["Vn0b", "Vn12b"], "QTs": ["QTs", "KTn"], "mixs": ["mixs"],
        "btf": ["btf01", "btf2"], "bttmp": ["bttmp"], "cin": ["cin", "bgat"], "csil": ["csil"], "siluT": ["siluT"], "gat": ["gat"],
    }
    B = {n: Buf(n, [RNG[t] for t in ts_]) for n, ts_ in GROUPS.items()}
    b_xin = [Buf("xin0", [RNG["xin0"]]), Buf("xin1", [RNG["xin1"]])]; b_xnb = [Buf("xnb0", [RNG["xnb0"]]), Buf("xnb1", [RNG["xnb1"]])]
    b_w = [Buf("w0", [RNG["wbuf0"]]), Buf("w1", [RNG["wbuf1"]])]
    b_E = [Buf(f"E{i}", [RNG[f"E{i}"]]) for i in range(3)]; b_stage = [Buf(f"st{i}", [RNG[f"stage{i}"]]) for i in range(2)]
    b_sg = [Buf(f"sg{i}", [RNG[f"sg{i}"]]) for i in range(2)]
    b_x1t = [Buf(f"x1t{i}", [(r2 + i * 4096, r2 + (i + 1) * 4096)]) for i in range(8)]
    b_small = [Buf(f"small{i}") for i in range(8)]
    b_hT = [[Buf(f"hT{i}_{p}") for p in range(2)] for i in range(8)]
    HTR = [b for pr in b_hT for b in pr]
    cnt = {"w": 0, "xin": 0, "E": 0, "stage": 0, "pb": 0}
    outs_done = []

    def mm(out, lhsT, rhs, start, stop, **kw):
        return lambda e: e.matmul(out, lhsT=lhsT, rhs=rhs, start=start, stop=stop, **kw)

    def act_fn(out, in_, func, **kw):
        return lambda e: e.activation(out=out, in_=in_, func=func, **kw)

    def tt(out, in0, in1, op):
        return lambda e: e.tensor_tensor(out=out, in0=in0, in1=in1, op=op)

    def ts(out, in0, s1, s2, op0, op1=None):
        if op1 is None:
            return lambda e: e.tensor_scalar(out=out, in0=in0, scalar1=s1, scalar2=None, op0=op0)
        return lambda e: e.tensor_scalar(out=out, in0=in0, scalar1=s1, scalar2=s2, op0=op0, op1=op1)

    def stt(out, in0, scalar, in1, op0, op1):
        return lambda e: e.scalar_tensor_tensor(out=out, in0=in0, scalar=scalar, in1=in1, op0=op0, op1=op1)

    def cp(out, in_):
        return lambda e: e.tensor_copy(out=out, in_=in_)

    epsc = vecs[:, 78:79]

    def rstd_ops(dst, src, scale, R, W):
        act.op(act_fn(dst, src, AF.Ln, scale=scale, bias=vecs[0:dst.shape[0], 78:79]), R=R + [B["vecs"]], W=W)
        act.op(act_fn(dst, dst, AF.Exp, scale=-0.5), R=W, W=W)

    piece = {}
    pending_wb = []

    def pbuf_of(key):
        if key not in piece:
            piece[key] = Buf("piece%d" % len(piece))
        return piece[key]

    def flush_wb():
        while pending_wb:
            o_, i_, rb, wbf = pending_wb.pop(0)
            sp.dma(o_, i_, R=[rb], W=[wbf])

    def load_w(src_list):
        s = cnt["w"] % 2
        cnt["w"] += 1
        flush_wb()
        for item in src_list(s):
            d, a = item[0], item[1]
            rb = [item[2]] if (len(item) > 2 and item[2] is not None) else []
            pool.dma(d, a, R=rb, W=[b_w[s]])
            if len(item) > 3 and item[3] is not None:
                pending_wb.append(item[3])
        return s

    sp.dma(ident_f[:], identd, W=[B["const"]])
    sp.dma(vecs[:, 0:48], b_adaT, W=[B["vecs"]]); sp.dma(vecs[:, 48:56], n1gT, W=[B["vecs"]]); sp.dma(vecs[:, 56:64], n2gT, W=[B["vecs"]])
    sp.dma(vecs[:, 64:70], convwT.rearrange("p c k -> p (c k)"), W=[B["vecs"]]); sp.dma(vecs[:, 70:72], gncT, W=[B["vecs"]]); sp.dma(vecs[:, 72:78], gnaT, W=[B["vecs"]])
    sp.dma(fg[:], fgb, W=[B["gp"]])
    dve.op(lambda e: e.memset(vecs[:, 78:79], EPS), W=[B["vecs"]])
    dve.op(lambda e: e.memset(ones_b[:], 1.0), W=[B["const"]])
    dve.op(cp(ident_b[:], ident_f[:]), R=[B["const"]], W=[B["const"]])
    dve.op(lambda e: e.memset(Ucar[:], 0.0), W=[B["Ucar"]])
    for g in range(3):
        pool.op(lambda e, g=g: e.memset(KT[g][:], 0.0), W=[B["KT"]])
        pool.op(lambda e, g=g: e.memset(VP[g][:], 0.0), W=[B["VP"]])
    sp.dma(btf01[:], bt01, W=[B["btf"]]); sp.dma(btf2[:], bt2, W=[B["btf"]])
    for (f, eb) in ((btf01, expb[0]), (btf2, expb[1])):
        act.op(act_fn(eb[:].rearrange("p a b -> p (a b)"), f[:].rearrange("p a b -> p (a b)"), AF.Exp), R=[B["btf"]], W=[B["const"]])
    v01 = btf01[:].rearrange("p h (o q) -> p h o q", o=2); v2 = btf2[:].rearrange("p h (o q) -> p h o q", o=3)
    for hp in range(2):
        for cp_ in range(2):
            dve.op(cp(bts[0][:, cp_, :, :, hp], v01[:, 2 * cp_ + hp, :, 0:4]), R=[B["btf"]], W=[B["const"]])
        for rho in range(4):
            dve.op(cp(bts[1][:, rho, :, :, hp], v01[:, 4 + hp:8:2, :, 0]), R=[B["btf"]], W=[B["const"]])
            dve.op(cp(bts[2][:, rho, :, :, hp], v2[:, hp:4:2, :, 0]), R=[B["btf"]], W=[B["const"]])
    pending_copies = []
    for g in (2, 1, 0):
        L = LG[g]
        for b in range(16):
            for r0 in range(4, L, 64):
                r1 = min(L, r0 + 64)
                pending_copies.append((kvs[g][b, r0 - 4:r1 - 4, :], kvin[g][b, r0:r1, :]))

    def issue_copies(n):
        for _ in range(n):
            if pending_copies:
                o_, i_ = pending_copies.pop(0)
                outs_done.append(sp.dma(o_, i_))
    sp.dma(cin[:], cpp, W=[B["cin"]]); sp.dma(bgat[:], bga, W=[B["cin"]])
    act.op(act_fn(csil[:], cin[:], AF.Silu), R=[B["cin"]], W=[B["csil"]])
    pe.group([lambda e, c=c: e.transpose(out=pbank[0][:, c * 17:(c + 1) * 17], in_=csil[:, c * 128:(c + 1) * 128], identity=ident_f[0:17, 0:17]) for c in range(8)],
             R=[B["csil"], B["const"]], W=[pbuf[0]])
    dve.op(cp(siluT[:].rearrange("p c b -> p (c b)"), pbank[0][:, 0:136]), R=[pbuf[0]], W=[B["siluT"]])
    MODCH = list(range(0, 16)) + list(range(24, 40))
    for blk in range(12):
        s = load_w(lambda s, blk=blk: [(wbuf[s][:], w_ada[:, blk * 512:(blk + 1) * 512].rearrange("(c p) n -> p c n", p=128))])
        if blk in (4, 5, 10, 11):
            col = (blk - 4) * 512 if blk < 6 else 1024 + (blk - 10) * 512
            pb = 1
            pe.group([mm(pbank[pb][0:17, :], siluT[:, k, :], wbuf[s][:, k, :], k == 0, k == 7) for k in range(8)], R=[B["siluT"], b_w[s]], W=[pbuf[pb]])
            dve.op(tt(gat[:, col:col + 512], pbank[pb][0:17, :], bgat[:, col:col + 512], ALU.add), R=[pbuf[pb], B["cin"]], W=[B["gat"]])
        else:
            pb = 2 + (blk % 2)
            fns = []
            for cc in range(4):
                for k in range(8):
                    fns.append(mm(pbank[pb][:, cc * 17:(cc + 1) * 17], wbuf[s][:, k, cc * 128:(cc + 1) * 128], siluT[:, k, :], k == 0, k == 7))
            pe.group(fns, R=[B["siluT"], b_w[s]], W=[pbuf[pb]])
            for cc in range(4):
                ch = blk * 4 + cc
                mi = MODCH.index(ch)
                dve.op(ts(modT[:, mi, :], pbank[pb][:, cc * 17:(cc + 1) * 17], vecs[:, ch:ch + 1], None, ALU.add), R=[pbuf[pb], B["vecs"]], W=[B["modT"]])
    for c in range(8):
        dve.op(ts(a1T[:, c, :], modT[:, 8 + c, :], 1.0, vecs[:, 48 + c:49 + c], ALU.add, ALU.mult), R=[B["modT"], B["vecs"]], W=[B["aT"]])
        dve.op(ts(a2T[:, c, :], modT[:, 24 + c, :], 1.0, vecs[:, 56 + c:57 + c], ALU.add, ALU.mult), R=[B["modT"], B["vecs"]], W=[B["aT"]])
    sp.dma(mod_scr, gat[:], R=[B["gat"]], W=[b_modscr])
    sp.dma(g1p[:], mod_scr[0:1, 0:1024].partition_broadcast(128), R=[b_modscr], W=[B["gp"]])
    sp.dma(g2p[:], mod_scr[0:1, 1024:2048].partition_broadcast(128), R=[b_modscr], W=[B["gp"]])

    def norm_to_hT(src_tile, npart, ntiles, sample, aT, shoff, As, SHs):
        def stage1(t):
            xa, xb_ = src_tile(t)
            s = t % 2
            act.op(act_fn(junk[0:npart, :], xa, AF.Square, scale=1.0 / 32.0, accum_out=small[0:npart, t:t + 1]), R=[xb_], W=[B["junk"], b_small[t]])
            rstd_ops(small[0:npart, 16 + t:17 + t], small[0:npart, t:t + 1], 1.0, [b_small[t]], [b_small[t]])
            dve.op(ts(xnb[s][0:npart, :], xa, small[0:npart, 16 + t:17 + t], None, ALU.mult), R=[xb_, b_small[t]], W=[b_xnb[s]])

        def stage2(t):
            s = t % 2
            pb = 4 + (t % 2)
            tp = pbank[pb][:].bitcast(BF16)
            pe.group([lambda e, c=c, tp=tp, s=s: e.transpose(out=tp[:, c * 128:c * 128 + npart], in_=xnb[s][0:npart, c * 128:(c + 1) * 128], identity=ident_b[0:npart, 0:npart]) for c in range(8)],
                     R=[b_xnb[s], B["const"]], W=[pbuf[pb]])
            for c in range(8):
                src = tp[:, c * 128:c * 128 + npart]
                dst = hT[:, c, t * 128:t * 128 + npart]
                eng = dve if t % 2 == 0 else act
                if not sample:
                    if eng is dve:
                        dve.op(ts(dst, src, aT[:, c, 0:1], modT[:, shoff + c, 0:1], ALU.mult, ALU.add), R=[pbuf[pb], B["aT"], B["modT"]], W=[b_hT[t][0]])
                    else:
                        act.op(act_fn(dst, src, AF.Identity, scale=aT[:, c, 0:1], bias=modT[:, shoff + c, 0:1]), R=[pbuf[pb], B["aT"], B["modT"]], W=[b_hT[t][0]])
                else:
                    dve.op(tt(sgt[0][:, 0:npart], src, As[:, c, :], ALU.mult), R=[pbuf[pb], B["samp"]], W=[b_sg[0]])
                    dve.op(tt(dst, sgt[0][:, 0:npart], SHs[:, c, :], ALU.add), R=[b_sg[0], B["samp"]], W=[b_hT[t][0]])
        stage1(0)
        for t in range(ntiles):
            if t + 1 < ntiles:
                stage1(t + 1)
            stage2(t)

    use_scr = {"on": False}

    def wsrc(w, c0, n, s, dst0=0):
        scr = wsc[w][:, c0:c0 + n].rearrange("(c p) n -> p c n", p=128)
        dst = wbuf[s][:, :, dst0:dst0 + n]
        pb_ = pbuf_of((w, c0, n))
        if not use_scr["on"]:
            return (dst, WSRC[w][:, c0:c0 + n].rearrange("(c p) n -> p c n", p=128), None, (scr, dst, b_w[s], pb_))
        return (dst, scr, pb_, None)

    def conv_part1(hi, tok0, ntok, nb, tb, sample):
        for c in range(2):
            uv = U[:, c, 0:nb * (tb + 2)].rearrange("p (b t) -> p b t", b=nb)
            zv = Zt[:, hi, c, 0:ntok].rearrange("p (b t) -> p b t", b=nb)
            gbv = GB[:, c, 0:ntok].rearrange("p (b t) -> p b t", b=nb)
            w0 = vecs[:, 64 + 3 * c:65 + 3 * c]; w1 = vecs[:, 65 + 3 * c:66 + 3 * c]; w2 = vecs[:, 66 + 3 * c:67 + 3 * c]
            dve.op(ts(zv, uv[:, :, 0:tb], w0, None, ALU.mult), R=[B["U"], B["vecs"]], W=[B["Zt"]])
            dve.op(stt(zv, uv[:, :, 1:tb + 1], w1, zv, ALU.mult, ALU.add), R=[B["U"], B["Zt"], B["vecs"]], W=[B["Zt"]])
            dve.op(stt(zv, uv[:, :, 2:tb + 2], w2, zv, ALU.mult, ALU.add), R=[B["U"], B["Zt"], B["vecs"]], W=[B["Zt"]])
            dve.op(tt(zv, zv, gbv, ALU.mult), R=[B["GB"], B["Zt"]], W=[B["Zt"]])
            act.op(act_fn(sq[:, 2 * hi + c, 0:ntok], Zt[:, hi, c, 0:ntok], AF.Square), R=[B["Zt"]], W=[B["sq"]])

    def conv_part2(hi, tok0, ntok):
        pe.group([mm(pbank[6][:, 0:ntok], ones_b[:], sq[:, 2 * hi + c, 0:ntok], c == 0, c == 1) for c in range(2)], R=[B["sq"], B["const"]], W=[pbuf[6]])
        rstd_ops(rsb[:, 0:ntok], pbank[6][:, 0:ntok], 1.0 / 256.0, [pbuf[6]], [B["rsb"]])
        for c in range(2):
            dve.op(stt(convn[:, c, tok0:tok0 + ntok], Zt[:, hi, c, 0:ntok], vecs[:, 70 + c:71 + c], rsb[:, 0:ntok], ALU.mult, ALU.mult),
                   R=[B["Zt"], B["rsb"], B["vecs"]], W=[B["convn"]])

    def proj_in(st, halves, sample):
        nb, tb = (16, 4) if sample else (1, 512)
        sA = load_w(lambda s: [wsrc("w_in", 0, 512, s)])
        sB = load_w(lambda s: [wsrc("w_in", 512, 256, s)])
        for hi, (tok0, ntok) in enumerate(halves):
            for cc in range(4):
                pb = cc % 2
                pe.group([mm(pbank[pb][:, 0:ntok], wbuf[sA][:, k, cc * 128:(cc + 1) * 128], hT[:, k, tok0:tok0 + ntok], k == 0, k == 7) for k in range(8)],
                         R=HTR + [b_w[sA]], W=[pbuf[pb]])
                dstt, bb = (GB, B["GB"]) if cc < 2 else (GC, B["GC"])
                act.op(act_fn(dstt[:, cc % 2, 0:ntok], pbank[pb][:, 0:ntok], AF.Copy), R=[pbuf[pb]], W=[bb])
            for c in range(2):
                pb = c % 2
                pe.group([mm(pbank[pb][:, 0:ntok], wbuf[sB][:, k, c * 128:(c + 1) * 128], hT[:, k, tok0:tok0 + ntok], k == 0, k == 7) for k in range(8)],
                         R=HTR + [b_w[sB]], W=[pbuf[pb]])
                uv = U[:, c, 0:nb * (tb + 2)].rearrange("p (b t) -> p b t", b=nb)
                if not sample:
                    dve.op(cp(U[:, c, 0:2], Ucar[:, c, :]), R=[B["Ucar"]], W=[B["U"]])
                dve.op(tt(uv[:, :, 2:tb + 2], pbank[pb][:, 0:ntok].rearrange("p (b t) -> p b t", b=nb), GC[:, c, 0:ntok].rearrange("p (b t) -> p b t", b=nb), ALU.mult),
                       R=[pbuf[pb], B["GC"]], W=[B["U"]])
                if not sample:
                    dve.op(cp(Ucar[:, c, :], U[:, c, tb:tb + 2]), R=[B["U"]], W=[B["Ucar"]])
            conv_part1(hi, tok0, ntok, nb, tb, sample)
            if sample:
                for c in range(2):
                    uv = U[:, c, 0:96].rearrange("p (b t) -> p b t", b=16)
                    for t_ in range(2):
                        outs_done.append(sp.dma(convs[:, t_, c * 128:(c + 1) * 128].rearrange("b p -> p b"), uv[:, :, 4 + t_], R=[B["U"]], allow_slow_non_contiguous=True))
            elif st == NST - 1 and hi == 1:
                for c in range(2):
                    outs_done.append(sp.dma(convp[:, c * 128:(c + 1) * 128].rearrange("t p -> p t"), U[:, c, 512:514], R=[B["U"]], allow_slow_non_contiguous=True))
        for (c0, n, kind) in ((768, 512, "q01"), (1280, 256, "q2"), (1536, 512, "k01"), (2048, 256, "k2")):
            s = load_w(lambda s, c0=c0, n=n: [wsrc("w_in", c0, n, s)])
            for cc in range(n // 128):
                gi = (cc // 2) if kind[1:] == "01" else 2
                ch = cc % 2
                for (tok0, ntok) in halves:
                    pb = cnt["pb"] % 2
                    cnt["pb"] += 1
                    pe.group([mm(pbank[pb][:, 0:ntok], wbuf[s][:, k, cc * 128:(cc + 1) * 128], hT[:, k, tok0:tok0 + ntok], k == 0, k == 7) for k in range(8)],
                             R=HTR + [b_w[s]], W=[pbuf[pb]])
                    src = pbank[pb][:, 0:ntok]
                    eng = act if cnt["pb"] % 2 == 0 else dve
                    if sample:
                        dst = (QTs if kind[0] == "q" else KTn)[:, gi * 2 + ch, 0:ntok]
                        bb = B["QTs"]
                    else:
                        sg_ = SG[gi]
                        m0 = tok0 // sg_
                        mlen = ntok // sg_
                        if kind[0] == "q":
                            dst = QT[gi][:, ch, :, m0:m0 + mlen]; bb = B["QT"]
                        else:
                            dst = KT[gi][:, ch, :, HIST[gi] + m0:HIST[gi] + m0 + mlen]; bb = B["KT"]
                        src = src.rearrange("p (m r) -> p r m", r=sg_)
                    scale = 0.125 if kind[0] == "q" else 1.0
                    if eng is act:
                        act.op(act_fn(dst, src, AF.Copy, scale=scale), R=[pbuf[pb]], W=[bb])
                    else:
                        dve.op(ts(dst, src, scale, None, ALU.mult), R=[pbuf[pb]], W=[bb])
        for hi, (tok0, ntok) in enumerate(halves):
            conv_part2(hi, tok0, ntok)
        for g in range(3):
            s = load_w(lambda s, g=g: [wsrc("w_in", 1536 + 256 * g, 256, s, 0), wsrc("w_in", 2304 + 256 * g, 256, s, 256)])
            if sample:
                pb = 2 + g % 2
                pe.group([mm(pbank[pb][0:64, :], hT[:, k, 0:64], wbuf[s][:, k, :], k == 0, k == 7) for k in range(8)], R=HTR + [b_w[s]], W=[pbuf[pb]])
                ss_ = cnt["stage"] % 2; cnt["stage"] += 1
                dve.op(cp(stage[ss_][0:64, :], pbank[pb][0:64, :]), R=[pbuf[pb]], W=[b_stage[ss_]])
                sp.dma(kv_scr[:, g * 512:(g + 1) * 512], stage[ss_][0:64, :], R=[b_stage[ss_]], W=[b_kvscr])
                for b in range(16):
                    outs_done.append(sp.dma(kvs[g][b, LG[g] - 4:LG[g], :], stage[ss_][4 * b:4 * b + 4, :], R=[b_stage[ss_]]))
                continue
            sg_ = SG[g]
            nblk_cur = CUR[g] // 128
            hb = HIST[g] // 128
            for r in range(sg_):
                for mb in range(nblk_cur):
                    pos0 = sg_ * mb * 128 + r
                    gpos = st * T + pos0
                    need_k = (gpos + sg_ * 127) >= S - LG[g]
                    ncol = 512 if need_k else 256
                    c0 = 0 if need_k else 256
                    pb = 2 + cnt["pb"] % 2
                    cnt["pb"] += 1
                    pe.group([mm(pbank[pb][:, 0:ncol], hT[:, k, pos0:pos0 + sg_ * 127 + 1:sg_], wbuf[s][:, k, c0:c0 + ncol], k == 0, k == 7) for k in range(8)],
                             R=HTR + [b_w[s]], W=[pbuf[pb]])
                    if not need_k:
                        act.op(act_fn(VP[g][:, r, hb + mb, :], pbank[pb][:, 0:256], AF.Copy), R=[pbuf[pb]], W=[B["VP"]])
                    else:
                        ss_ = cnt["stage"] % 2; cnt["stage"] += 1
                        dve.op(cp(stage[ss_][:], pbank[pb][:, 0:512]), R=[pbuf[pb]], W=[b_stage[ss_]])
                        act.op(act_fn(VP[g][:, r, hb + mb, :], stage[ss_][:, 256:512], AF.Copy), R=[b_stage[ss_]], W=[B["VP"]])
                        row0 = gpos - (S - LG[g])
                        dst = kvp[g][row0:row0 + sg_ * 127 + 1:sg_, :]
                        outs_done.append(sp.dma(dst, stage[ss_][:], R=[b_stage[ss_]]))

    def attention_prompt(st):
        for g in range(3):
            sg_ = SG[g]; hb = HIST[g] // 128; nq_blocks = CUR[g] // 128
            ebt = expb[0] if g < 2 else expb[1]
            jobs = []
            for r in range(sg_):
                for mb in range(nq_blocks):
                    ne = 0
                    for o in range(NBLK[g]):
                        if st * nq_blocks + (mb - o) >= 0:
                            ne += 1
                    for p in range(2):
                        odi = 6 + (cnt["pb"] % 2)
                        cnt["pb"] += 1
                        for hp in range(2):
                            jobs.append(dict(r=r, mb=mb, ne=ne, p=p, hp=hp, odi=odi, si=4 + (cnt["E"] % 2), ei=cnt["E"] % 3))
                            cnt["E"] += 1

            def emit_S(j):
                r, mb, ne, p, hp = j["r"], j["mb"], j["ne"], j["p"], j["hp"]
                h = 2 * p + hp
                hh = (4 * g + h) if g < 2 else h
                Sps = pbank[j["si"]]; Sb = pbuf[j["si"]]
                n = ne * 128
                fns = []
                for o in range(ne):
                    kcol = (hb + mb - o) * 128
                    fns.append(mm(Sps[:, o * 128:(o + 1) * 128], KT[g][hp * 64:(hp + 1) * 64, p, r, kcol:kcol + 128],
                                  QT[g][hp * 64:(hp + 1) * 64, p, r, mb * 128:(mb + 1) * 128], True, True, skip_group_check=True))
                pe.group(fns, R=[B["KT"], B["QT"]], W=[Sb])
                act.op(act_fn(Ebuf[j["ei"]][:, 0:n], Sps[:, 0:n], AF.Exp), R=[Sb], W=[b_E[j["ei"]]])
                dve.op(tt(Ebuf[j["ei"]][:, 0:n], Ebuf[j["ei"]][:, 0:n], ebt[:, hh, 0:n], ALU.mult), R=[b_E[j["ei"]], B["const"]], W=[b_E[j["ei"]]])

            def emit_PV(j):
                r, mb, ne, p, hp = j["r"], j["mb"], j["ne"], j["p"], j["hp"]
                h = 2 * p + hp
                od = pbank[j["odi"]]; odb = pbuf[j["odi"]]
                E_ = Ebuf[j["ei"]]
                fns = []
                for o in range(ne):
                    fns.append(mm(od[hp * 64:(hp + 1) * 64, 0:128], VP[g][:, r, hb + mb - o, h * 64:(h + 1) * 64], E_[:, o * 128:(o + 1) * 128],
                                  o == 0, o == ne - 1, tile_position=(0, hp * 64), skip_group_check=True))
                for o in range(ne):
                    fns.append(mm(od[hp * 64:(hp + 1) * 64, 128:256], ones_b[:, 0:64], E_[:, o * 128:(o + 1) * 128],
                                  o == 0, o == ne - 1, tile_position=(0, hp * 64), skip_group_check=True))
                pe.group(fns, R=[B["VP"], b_E[j["ei"]], B["const"]], W=[odb])
                if hp == 1:
                    t0 = sg_ * mb * 128 + r
                    sl = slice(t0, t0 + sg_ * 127 + 1, sg_)
                    dve.op(cp(AT[:, 2 * g + p, sl], od[:, 0:128]), R=[odb], W=[B["AT"]])
                    if g == 0:
                        dve.op(cp(Zb[:, p, sl], od[:, 128:256]), R=[odb], W=[B["Zb"]])
                    else:
                        dve.op(tt(Zb[:, p, sl], od[:, 128:256], Zb[:, p, sl], ALU.add), R=[odb, B["Zb"]], W=[B["Zb"]])
            emit_S(jobs[0])
            for i, j in enumerate(jobs):
                if i + 1 < len(jobs):
                    emit_S(jobs[i + 1])
                emit_PV(j)
            if st < NST - 1:
                if g < 2:
                    pool.op(cp(KT[g][:, :, :, 0:HIST[g]], KT[g][:, :, :, CUR[g]:CUR[g] + HIST[g]]), R=[B["KT"]], W=[B["KT"]])
                    pool.op(cp(VP[g][:, :, 0:hb, :], VP[g][:, :, nq_blocks:nq_blocks + hb, :]), R=[B["VP"]], W=[B["VP"]])
                else:
                    for j in range(2):
                        pool.op(cp(KT[g][:, :, :, j * 128:(j + 1) * 128], KT[g][:, :, :, (j + 1) * 128:(j + 2) * 128]), R=[B["KT"]], W=[B["KT"]])
                        pool.op(cp(VP[g][:, :, j, :], VP[g][:, :, j + 1, :]), R=[B["VP"]], W=[B["VP"]])

    b_Zh = [Buf("Zh0"), Buf("Zh1")]; b_Ah = [Buf("Ah0"), Buf("Ah1")]; b_mix = [Buf("mix0"), Buf("mix1")]

    def combine(ntok_total, halves, mixdst, mixb):
        for h, (tok0, ntok) in enumerate(halves):
            sl = slice(tok0, tok0 + ntok)
            for p in range(2):
                act.op(act_fn(Zb[:, p, sl], Zb[:, p, sl], AF.Ln), R=[B["Zb"], b_Zh[h]], W=[b_Zh[h]])
                act.op(act_fn(Zb[:, p, sl], Zb[:, p, sl], AF.Exp, scale=-1.0), R=[B["Zb"], b_Zh[h]], W=[b_Zh[h]])
            for c in range(6):
                dve.op(tt(AT[:, c, sl], AT[:, c, sl], Zb[:, c % 2, sl], ALU.mult), R=[B["AT"], B["Zb"], b_Zh[h], b_Ah[h]], W=[b_Ah[h]])
        for h, (tok0, ntok) in enumerate(halves):
            sl = slice(tok0, tok0 + ntok)
            for c in range(6):
                act.op(act_fn(sq[:, c, 0:ntok], AT[:, c, sl], AF.Square), R=[B["AT"], b_Ah[h]], W=[B["sq"]])
            pe.group([mm(pbank[6][:, 0:ntok], ones_b[:], sq[:, c, 0:ntok], c == 0, c == 5) for c in range(6)], R=[B["sq"], B["const"]], W=[pbuf[6]])
            rstd_ops(rsb[:, 0:ntok], pbank[6][:, 0:ntok], 1.0 / 768.0, [pbuf[6]], [B["rsb"]])
            for c in range(6):
                dve.op(stt(mixdst[:, c, sl], AT[:, c, sl], vecs[:, 72 + c:73 + c], rsb[:, 0:ntok], ALU.mult, ALU.mult),
                       R=[B["AT"], b_Ah[h], B["rsb"], B["vecs"], mixb], W=[b_mix[h]])

    def post(ntiles, npart, xsrc, ydst, sample, mixsrc, ga1, ga2, aT, As, SHs):
        ntok = ntiles * 128 if not sample else npart
        halves = [(0, 512), (512, 512)] if not sample else [(0, 64)]
        s0 = load_w(lambda s: [wsrc("w_out", 0, 512, s)])
        s1 = load_w(lambda s: [wsrc("w_out", 512, 512, s)])
        def wout_tile(t):
            xi = cnt["xin"] % 2; cnt["xin"] += 1
            sp.dma(xin[xi][0:npart, :], xsrc(t), W=[b_xin[xi]])
            for hf, s in ((0, s0), (1, s1)):
                pb = hf
                fns = []
                for c in range(8):
                    lhs = convn[:, c, t * 128:t * 128 + npart] if c < 2 else mixsrc[:, c - 2, t * 128:t * 128 + npart]
                    fns.append(mm(pbank[pb][0:npart, :], lhs, wbuf[s][:, c, :], c == 0, c == 7))
                pe.group(fns, R=[B["convn"], b_mix[(t * 128) // 512 if not sample else 0], b_w[s]], W=[pbuf[pb]])
                xs_ = x1[0:npart, t, hf * 512:(hf + 1) * 512]
                dve.op(tt(xs_, pbank[pb][0:npart, :], ga1[0:npart, hf * 512:(hf + 1) * 512], ALU.mult), R=[pbuf[pb], B["gp"], B["samp"]], W=[b_x1t[t]])
                dve.op(tt(xs_, xs_, xin[xi][0:npart, hf * 512:(hf + 1) * 512], ALU.add), R=[b_xin[xi]], W=[b_x1t[t]])
            return x1[0:npart, t, :], b_x1t[t]
        norm_to_hT(wout_tile, npart, ntiles, sample, aT, 16, As, SHs)
        if not sample:
            prefetch_next_x()
        if pre_ffn_hook["fn"] is not None and not sample:
            pre_ffn_hook["fn"]()
            pre_ffn_hook["fn"] = None
        blks = [(pi, jb, j0 + 2 * jb) for pi, (j0, nj) in enumerate(FFPARTS) for jb in range(nj // 2)]

        def ld(i):
            issue_copies(8)
            ja = blks[i][2]
            return load_w(lambda s, ja=ja: [wsrc("w_gate", ja * 128, 256, s, 0), wsrc("w_up", ja * 128, 256, s, 256)])
        slots = {0: ld(0)}
        bi = 0
        for pi, (j0, nj) in enumerate(FFPARTS):
            for jb in range(nj // 2):
                if bi + 1 < len(blks):
                    slots[bi + 1] = ld(bi + 1)
                if jb == 0:
                    wd_scr = wsc["w_down"][j0 * 128:(j0 + nj) * 128, :].rearrange("(j p) n -> p j n", p=128)
                    wd_pb = pbuf_of(("w_down", j0))
                    if use_scr["on"]:
                        pool.dma(wd[:, 0:nj, :], wd_scr, R=[wd_pb], W=[B["wd"]])
                    else:
                        pool.dma(wd[:, 0:nj, :], w_down[j0 * 128:(j0 + nj) * 128, :].rearrange("(j p) n -> p j n", p=128), W=[B["wd"]])
                        pending_wb.append((wd_scr, wd[:, 0:nj, :], B["wd"], wd_pb))
                s = slots[bi]
                bi += 1
                for jj in range(2):
                    for (tok0, nt_) in halves:
                        pe.group([mm(pbank[0][:, 0:nt_], wbuf[s][:, k, jj * 128:(jj + 1) * 128], hT[:, k, tok0:tok0 + nt_], k == 0, k == 7) for k in range(8)],
                                 R=HTR + [b_w[s]], W=[pbuf[0]])
                        pe.group([mm(pbank[1][:, 0:nt_], wbuf[s][:, k, 256 + jj * 128:256 + (jj + 1) * 128], hT[:, k, tok0:tok0 + nt_], k == 0, k == 7) for k in range(8)],
                                 R=HTR + [b_w[s]], W=[pbuf[1]])
                        si = cnt["pb"] % 2; cnt["pb"] += 1
                        act.op(act_fn(sgt[si][:, 0:nt_], pbank[0][:, 0:nt_], AF.Silu), R=[pbuf[0]], W=[b_sg[si]])
                        dve.op(tt(actT[:, 2 * jb + jj, tok0:tok0 + nt_], sgt[si][:, 0:nt_], pbank[1][:, 0:nt_], ALU.mult), R=[b_sg[si], pbuf[1]], W=[B["actT"]])
            for t in range(ntiles):
                for hf in range(2):
                    pb = 2 + hf
                    pe.group([mm(pbank[pb][0:npart, :], actT[:, j, t * 128:t * 128 + npart], wd[:, j, hf * 512:(hf + 1) * 512], j == 0, j == nj - 1) for j in range(nj)],
                             R=[B["actT"], B["wd"]], W=[pbuf[pb]])
                    si = cnt["pb"] % 2; cnt["pb"] += 1
                    dve.op(tt(sgt[si][0:npart, :], pbank[pb][0:npart, :], ga2[0:npart, hf * 512:(hf + 1) * 512], ALU.mult), R=[pbuf[pb], B["gp"], B["samp"]], W=[b_sg[si]])
                    xs_ = x1[0:npart, t, hf * 512:(hf + 1) * 512]
                    dve.op(tt(xs_, xs_, sgt[si][0:npart, :], ALU.add), R=[b_sg[si]], W=[b_x1t[t]])
        for t in range(ntiles):
            xa = x1[0:npart, t, :]
            act.op(act_fn(junk[0:npart, :], xa, AF.Square, scale=1.0 / 32.0, accum_out=small[0:npart, 32 + t:33 + t]), R=[b_x1t[t]], W=[B["junk"], b_small[t]])
            rstd_ops(small[0:npart, 48 + t:49 + t], small[0:npart, 32 + t:33 + t], 1.0, [b_small[t]], [b_small[t]])
            dve.op(stt(xa, xa, small[0:npart, 48 + t:49 + t], fg[0:npart, :], ALU.mult, ALU.mult), R=[b_small[t], B["gp"]], W=[b_x1t[t]])
            outs_done.append(sp.dma(ydst(t), xa, R=[b_x1t[t]]))

    hoisted = {"done": False}
    pre_ffn_hook = {"fn": None}

    def sample_prep():
        hoisted["done"] = True
        for (dst, srcT, off) in ((A1s, a1T, None), (A2s, a2T, None), (SH1s, modT, 0), (SH2s, modT, 16)):
            for c in range(8):
                src = srcT[:, c, 1:17] if off is None else modT[:, off + c, 1:17]
                dve.op(cp(dst[:, c, :].rearrange("p (b t) -> p b t", t=4), src.unsqueeze(2).to_broadcast([128, 16, 4])), R=[B["aT"], B["modT"]], W=[B["samp"]])
        for b in range(16):
            sp.dma(g1s[4 * b:4 * b + 4, :], mod_scr[1 + b:2 + b, 0:1024].partition_broadcast(4), R=[b_modscr], W=[B["samp"]])
            sp.dma(g2s[4 * b:4 * b + 4, :], mod_scr[1 + b:2 + b, 1024:2048].partition_broadcast(4), R=[b_modscr], W=[B["samp"]])

    prefetched = {}
    cur_st = {"st": None}
    st_seq = list(st_list if st_list is not None else range(nst))

    def prefetch_next_x():
        st = cur_st["st"]
        if st is None or st not in st_seq:
            return
        i = st_seq.index(st)
        if i + 1 >= len(st_seq):
            return
        nst_ = st_seq[i + 1]
        for t in range(2):
            xi = cnt["xin"] % 2; cnt["xin"] += 1
            sp.dma(xin[xi][:], xp[nst_ * T + t * 128: nst_ * T + (t + 1) * 128, :], W=[b_xin[xi]])
            prefetched[(nst_, t)] = xi

    for st in (st_list if st_list is not None else range(nst)):
        def src_tile(t, st=st):
            if (st, t) in prefetched:
                xi = prefetched.pop((st, t))
                return xin[xi][:], b_xin[xi]
            xi = cnt["xin"] % 2; cnt["xin"] += 1
            sp.dma(xin[xi][:], xp[st * T + t * 128: st * T + (t + 1) * 128, :], W=[b_xin[xi]])
            return xin[xi][:], b_xin[xi]
        cur_st["st"] = st
        if do_sample and st == st_seq[-1]:
            pre_ffn_hook["fn"] = sample_prep
        norm_to_hT(src_tile, 128, 8, False, a1T, 0, None, None)
        proj_in(st, [(0, 512), (512, 512)], False)
        attention_prompt(st)
        combine(T, [(0, 512), (512, 512)], QTall, B["QT"])
        post(8, 128, lambda t, st=st: xp[st * T + t * 128: st * T + (t + 1) * 128, :], lambda t, st=st: yp[st * T + t * 128: st * T + (t + 1) * 128, :],
             False, QTall, g1p, g2p, a2T, None, None)
        flush_wb()
        use_scr["on"] = True

    def sample_pass():
        if not hoisted["done"]:
            sample_prep()
        for c in range(2):
            for t_ in range(2):
                sp.dma(U[:, c, 0:96].rearrange("p (b t) -> p b t", b=16)[:, :, t_], sconv[:, t_, c * 128:(c + 1) * 128].rearrange("b p -> p b"), W=[B["U"]], allow_slow_non_contiguous=True)

        def src_tile_s(t):
            sp.dma(xin[0][0:64, :], xs, W=[b_xin[0]])
            return xin[0][0:64, :], b_xin[0]
        norm_to_hT(src_tile_s, 64, 1, True, None, 0, A1s, SH1s)
        proj_in(0, [(0, 64)], True)
        dve.op(lambda e: e.memset(Qbd[:], 0.0), W=[B["Qbd"]])
        for i_ in (4, 5):
            dve.op(lambda e, i_=i_: e.memset(pbank[i_][:], 0.0), W=[pbuf[i_]])
        for hp in range(2):
            dve.op(cp(Qbd[hp * 64:(hp + 1) * 64, :, :, hp], QTs[hp * 64:(hp + 1) * 64, :, :]), R=[B["QTs"]], W=[B["Qbd"]])
        b_Es2 = [B["Es2_0"], B["Es2_1"]]
        NCOL = (32, 32, 48)
        NO = NBLK

        def scol(g, a, cp_, o):
            if g == 0:
                return cp_ * 16 + o * 8
            return a * (4 * NO[g]) + cp_ * (2 * NO[g]) + o * 2

        def oslot(c, b):
            bank, cc = (6, c) if c < 4 else (2, c - 4)
            return bank, cc * 128 + b * 8

        def dslot(g, b):
            bank, gg = (7, g) if g < 2 else (3, 0)
            return bank, gg * 256 + b * 16
        for b in range(16):
            par = b % 2
            Vn0_ = Vn0 if par == 0 else Vn0b
            Vn12_ = Vn12 if par == 0 else Vn12b
            vnb = B["Vnb%d" % par]
            pool.dma(Vn0_[:], kv_scr[4 * b:4 * b + 4, 256:512], R=[b_kvscr], W=[vnb])
            pool.dma(Vn12_[:, :, 0, :], kv_scr[4 * b:4 * b + 4, 768:1024].rearrange("(o n) c -> o n c", o=1), R=[b_kvscr], W=[vnb])
            pool.dma(Vn12_[:, :, 1, :], kv_scr[4 * b:4 * b + 4, 1280:1536].rearrange("(o n) c -> o n c", o=1), R=[b_kvscr], W=[vnb])
            bk = blkK[par]; bkb = B["blk%d" % par]
            KTs = KTs2[par]; Vs = Vs2[par]
            ktb = B["KTs%d" % par]; vsb = B["Vs%d" % par]
            sp.dma(bk[:, 0, :], kvin[0][b][:, 0:256], W=[bkb])
            sp.dma(bk[:, 1:5, :], kvin[1][b].rearrange("(i r) c -> i r c", r=4)[:, :, 0:256], W=[bkb])
            for k_ in range(2):
                sp.dma(bk[:, 5 + k_:13:2, :], kvin[2][b, k_ * 1024:(k_ + 1) * 1024, :].rearrange("(i r) c -> i r c", r=8)[:, 0:4, 0:256], W=[bkb])
            pool.dma(Vs[:, 0, :], kvin[0][b][:, 256:512], W=[vsb])
            pool.dma(Vs[:, 1:5, :], kvin[1][b].rearrange("(i r) c -> i r c", r=4)[:, :, 256:512], W=[vsb])
            for k_ in range(2):
                pool.dma(Vs[:, 5 + k_:13:2, :], kvin[2][b, k_ * 1024:(k_ + 1) * 1024, :].rearrange("(i r) c -> i r c", r=8)[:, 0:4, 256:512], W=[vsb])
            for q4 in range(7):
                pb = q4 % 2
                idx = [(blk, p) for blk in range(13) for p in range(2)][q4 * 4:(q4 + 1) * 4]
                pe.group([lambda e, j=j, blk=blk, p=p, pb=pb, bk=bk: e.transpose(out=pbank[pb][:, j * 128:(j + 1) * 128], in_=bk[:, blk, p * 128:(p + 1) * 128], identity=ident_f[:])
                          for j, (blk, p) in enumerate(idx)], R=[bkb, B["const"]], W=[pbuf[pb]])
                nn = len(idx)
                dve.op(cp(KTs[:].rearrange("p a b k -> p (a b k)")[:, q4 * 512:q4 * 512 + nn * 128], pbank[pb][:, 0:nn * 128]), R=[pbuf[pb]], W=[ktb])

            def blk_of(g, rho, o):
                if g == 0:
                    return 0
                return (1 + rho) if g == 1 else (5 + 2 * rho + (2 - o))
            for g in range(3):
                Sps = pbank[4 + (g % 2)]; Sb = pbuf[4 + (g % 2)]
                E_ = Es2[g % 2]; Eb = b_Es2[g % 2]
                nc_ = NCOL[g]
                fns = []
                for cp_ in range(2):
                    c = 2 * g + cp_
                    if g == 0:
                        q = Qbd[:, c, 4 * b:4 * b + 4, :]
                        fns.append(mm(Sps[0:4, scol(0, 0, cp_, 0):scol(0, 0, cp_, 0) + 8], KTn[:, c, 4 * b:4 * b + 4], q, True, True, skip_group_check=True))
                        fns.append(mm(Sps[:, scol(0, 0, cp_, 1):scol(0, 0, cp_, 1) + 8], KTs[:, 0, cp_, :], q, True, True, skip_group_check=True))
                    else:
                        for rho in range(4):
                            q = Qbd[:, c, 4 * b + rho, :]
                            c0 = scol(g, rho, cp_, 0)
                            fns.append(mm(Sps[0:1, c0:c0 + 2], KTn[:, c, 4 * b + rho:4 * b + rho + 1], q, True, True, skip_group_check=True))
                            for o in range(1, NO[g]):
                                c1 = scol(g, rho, cp_, o)
                                fns.append(mm(Sps[:, c1:c1 + 2], KTs[:, blk_of(g, rho, o), cp_, :], q, True, True, skip_group_check=True))
                pe.group(fns, R=[B["Qbd"], B["QTs"], ktb], W=[Sb])
                si = g % 2
                btf_ = bts[g][:].rearrange("p a b c d -> p (a b c d)")
                dve.op(tt(sgt[si][:, 0:nc_], Sps[:, 0:nc_], btf_, ALU.add), R=[Sb, B["const"]], W=[b_sg[si]])
                act.op(act_fn(E_[:, 0:nc_], sgt[si][:, 0:nc_], AF.Exp), R=[b_sg[si]], W=[Eb])
                fns = []
                wb = set()
                for cp_ in range(2):
                    c = 2 * g + cp_
                    obank, ocol = oslot(c, b)
                    wb.add(obank)
                    if g == 0:
                        dst = pbank[obank][:, ocol:ocol + 8]
                        c0 = scol(0, 0, cp_, 0); c1 = scol(0, 0, cp_, 1)
                        fns.append(mm(dst, Vn0_[0:4, cp_ * 128:(cp_ + 1) * 128], E_[0:4, c0:c0 + 8], True, False, skip_group_check=True))
                        fns.append(mm(dst, Vs[:, 0, cp_ * 128:(cp_ + 1) * 128], E_[:, c1:c1 + 8], False, True, skip_group_check=True))
                    else:
                        for rho in range(4):
                            dst = pbank[obank][:, ocol + 2 * rho:ocol + 2 * rho + 2]
                            c0 = scol(g, rho, cp_, 0)
                            fns.append(mm(dst, Vn12_[0:1, rho, g - 1, cp_ * 128:(cp_ + 1) * 128], E_[0:1, c0:c0 + 2], True, False, skip_group_check=True))
                            for o in range(1, NO[g]):
                                c1 = scol(g, rho, cp_, o)
                                fns.append(mm(dst, Vs[:, blk_of(g, rho, o), cp_ * 128:(cp_ + 1) * 128], E_[:, c1:c1 + 2], False, o == NO[g] - 1, skip_group_check=True))
                dbank, dcol = dslot(g, b)
                wb.add(dbank)
                dd = pbank[dbank][:, dcol:dcol + 16]
                if g == 0:
                    Ev = E_[:, 0:32].rearrange("p (c o k) -> p c o k", c=2, o=2)
                    fns.append(mm(dd, ones_b[0:4, :], Ev[0:4, :, 0, :], True, False, skip_group_check=True))
                    fns.append(mm(dd, ones_b[:, :], Ev[:, :, 1, :], False, True, skip_group_check=True))
                else:
                    Ev = E_[:, 0:nc_].rearrange("p (r c o h) -> p r c o h", r=4, c=2, o=NO[g])
                    fns.append(mm(dd, ones_b[0:1, :], Ev[0:1, :, :, 0, :], True, False, skip_group_check=True))
                    for o in range(1, NO[g]):
                        fns.append(mm(dd, ones_b[:, :], Ev[:, :, :, o, :], False, o == NO[g] - 1, skip_group_check=True))
                pe.group(fns, R=[Eb, vsb, vnb, B["const"]], W=[pbuf[i] for i in sorted(wb)])
        for hp in range(2):
            rows = slice(hp * 64, (hp + 1) * 64)
            dve.op(cp(AT[rows, 0:4, 0:64].rearrange("p c (b q) -> p c b q", q=4),
                      pbank[6][rows, :].rearrange("p (c b q h) -> p c b q h", c=4, b=16, q=4)[:, :, :, :, hp]), R=[pbuf[6]], W=[B["AT"]])
            dve.op(cp(AT[rows, 4:6, 0:64].rearrange("p c (b q) -> p c b q", q=4),
                      pbank[2][rows, 0:256].rearrange("p (c b q h) -> p c b q h", c=2, b=16, q=4)[:, :, :, :, hp]), R=[pbuf[2]], W=[B["AT"]])
            zv = Zb[rows, :, 0:64].rearrange("p c (b q) -> p c b q", q=4)
            dve.op(cp(zv, pbank[7][rows, 0:256].rearrange("p (b c q h) -> p c b q h", b=16, c=2, q=4)[:, :, :, :, hp]), R=[pbuf[7]], W=[B["Zb"]])
            dve.op(tt(zv, zv, pbank[7][rows, 256:512].rearrange("p (b r c h) -> p c b r h", b=16, r=4, c=2)[:, :, :, :, hp], ALU.add), R=[pbuf[7], B["Zb"]], W=[B["Zb"]])
            dve.op(tt(zv, zv, pbank[3][rows, 0:256].rearrange("p (b r c h) -> p c b r h", b=16, r=4, c=2)[:, :, :, :, hp], ALU.add), R=[pbuf[3], B["Zb"]], W=[B["Zb"]])
        combine(64, [(0, 64)], mixs, B["mixs"])
        post(1, 64, lambda t: xs, lambda t: ys, True, mixs, g1s, g2s, None, A2s, SH2s)


    issue_copies(10 ** 6)
    if do_sample:
        sample_pass()

    for ev in outs_done:
        sp.wait(ev)
    sp.prog.append(("w", sp.dsems[0], 16 * sp.dcnt[0]))
    tail_ev = sp.dma(kv_scr[0:17, 0:1024], mod_scr[:, 0:1024], R=[b_modscr], W=[b_kvscr])
    sp.wait(tail_ev)
    tail_ev = sp.dma(kv_scr[32:49, 0:1024], mod_scr[:, 0:1024], R=[b_modscr], W=[b_kvscr])
    sp.wait(tail_ev)

    print("engine counts", {e.name: (e.n, len(e.prog), e.dcnt) for e in engs})
    with nc.Block() as block:
        @block.tensor
        def _(e):
            pe.replay(e)

        @block.scalar
        def _(e):
            act.replay(e)

        @block.vector
        def _(e):
            dve.replay(e)

        @block.gpsimd
        def _(e):
            pool.replay(e)

        @block.sync
        def _(e):
            sp.replay(e)
    es.close()
    return nc


_CACHE = {}


def kernel(x_prompt, x_sample, state_conv, cache_kv1, cache_kv2, cache_kv3, c_prompt, c_sample,
           w_ada, b_ada, norm1_g, norm2_g, w_in, conv_w, gn_conv, gn_attn, w_out,
           w_gate, w_up, w_down, rel_bias, final_g):
    f = lambda a: np.ascontiguousarray(np.asarray(a, dtype=np.float32))
    x_prompt = f(x_prompt); x_sample = f(x_sample)
    if "nc" not in _CACHE:
        _CACHE["nc"] = build_program()
    nc = _CACHE["nc"]
    rb_aug = np.concatenate([f(rel_bias), np.full((1, 12), NEG, np.float32)], 0)
    idx = _bias_index_tiles()
    bt01 = np.stack([rb_aug[idx[h // 4], h] for h in range(8)], 1).reshape(128, 8, 256)
    bt2 = np.stack([rb_aug[idx[2], 8 + h] for h in range(4)], 1).reshape(128, 4, 384)
    b_ada_ = f(b_ada)[0]
    shared = {
        "w_ada": f(w_ada)[0], "b_adaT": f(b_ada_.reshape(48, 128).T),
        "bga": f(np.broadcast_to(np.concatenate([b_ada_[2048:3072], b_ada_[5120:6144]])[None, :], (17, 2048))),
        "n1gT": f(f(norm1_g)[0].reshape(8, 128).T), "n2gT": f(f(norm2_g)[0].reshape(8, 128).T),
        "w_in": f(w_in)[0], "convwT": f(f(conv_w)[0].reshape(3, 2, 128).transpose(2, 1, 0)),
        "gncT": f(f(gn_conv)[0].reshape(2, 128).T), "gnaT": f(f(gn_attn)[0].reshape(6, 128).T),
        "w_out": f(w_out)[0], "w_gate": f(w_gate)[0], "w_up": f(w_up)[0], "w_down": f(w_down)[0],
        "fgb": f(np.broadcast_to(f(final_g)[None, :], (128, D))), "bt01": f(bt01), "bt2": f(bt2),
        "ident": np.eye(128, dtype=np.float32),
    }
    in_maps = []
    for i in range(NCORES):
        bs = slice(16 * i, 16 * i + 16)
        m = dict(shared)
        m["xp"] = x_prompt[i]
        m["xs"] = x_sample[bs].reshape(64, D)
        m["cc"] = f(np.concatenate([f(c_prompt)[i:i + 1], f(c_sample)[bs]], 0))
        m["sconv"] = f(state_conv)[0, bs]
        m["kv1"] = f(cache_kv1)[0, bs].reshape(16, 128, 512)
        m["kv2"] = f(cache_kv2)[0, bs].reshape(16, 512, 512)
        m["kv3"] = f(cache_kv3)[0, bs].reshape(16, 2048, 512)
        in_maps.append(m)
    res = run_bass_kernel_spmd(nc, in_maps, core_ids=list(range(NCORES)))
    R = res.results
    cat = lambda k: np.stack([np.asarray(r[k], dtype=np.float32) for r in R], 0)
    y_prompt = cat("yp").reshape(8, S, D)
    y_sample = cat("ys").reshape(128, 4, D)
    conv_p = cat("convp").reshape(1, 8, 2, 256)
    kv1p = cat("kv1p").reshape(1, 8, 128, 2, 4, 64)
    kv2p = cat("kv2p").reshape(1, 8, 512, 2, 4, 64)
    kv3p = cat("kv3p").reshape(1, 8, 2048, 2, 4, 64)
    conv_s = cat("convs").reshape(1, 128, 2, 256)
    kv1s = cat("kv1s").reshape(1, 128, 128, 2, 4, 64)
    kv2s = cat("kv2s").reshape(1, 128, 512, 2, 4, 64)
    kv3s = cat("kv3s").reshape(1, 128, 2048, 2, 4, 64)
    return (y_prompt, y_sample, conv_p, kv1p, kv2p, kv3p, conv_s, kv1s, kv2s, kv3s)
```

```python
import numpy as np
from contextlib import ExitStack
import concourse.bass as bass
import concourse.mybir as mybir
from concourse.bass_utils import run_bass_kernel_spmd

F32 = mybir.dt.float32
BF16 = mybir.dt.bfloat16
AF = mybir.ActivationFunctionType
ALU = mybir.AluOpType

NCORES = 8
D = 1024
S = 8192
T = 1024
NST = S // T
DFF = 2816
NFF = DFF // 128
EPS = 1e-6
NEG = -30000.0
SG = (1, 4, 8)
HIST = (128, 128, 256)
CUR = (1024, 256, 128)
NBLK = (2, 2, 3)
LG = (128, 512, 2048)
DIL = (1, 4, 16)
FFPARTS = ((0, 6), (6, 6), (12, 6), (18, 4))


def _t5_buckets(dist):
    dist = np.asarray(dist, np.int32)
    max_exact = 16
    large = max_exact + (np.log(np.maximum(dist, 1).astype(np.float32) / max_exact)
                         / np.log(2048 / max_exact) * (32 - max_exact)).astype(np.int32)
    large = np.minimum(large, 31)
    return np.where(dist < max_exact, dist, large).astype(np.int32)


def _bias_index_tiles():
    out = []
    k = np.arange(128)[:, None, None]
    for g in range(3):
        nb = NBLK[g]
        o = np.arange(nb)[None, :, None]
        i = np.arange(128)[None, None, :]
        delta = i + 128 * o - k
        if g < 2:
            valid = (delta >= 0) & (delta <= 128)
            j = np.clip(delta, 0, 128)
        else:
            valid = (delta >= 0) & (delta <= 256) & (delta % 2 == 0)
            j = np.clip(delta // 2, 0, 128)
        bidx = _t5_buckets(j * DIL[g])
        out.append(np.where(valid, bidx, 32).astype(np.int64))
    return out


ALLBUFS = []


class Buf:
    def __init__(self, name, ranges=()):
        self.name = name
        self.w = None
        self.r = {}
        self.ranges = list(ranges)
        self.ov = None
        self.is_psum = False
        ALLBUFS.append(self)

    def group(self):
        if self.ov is None:
            self.ov = [self]
            for o in ALLBUFS:
                if o is self:
                    continue
                if any(a < d and c < b for (a, b) in self.ranges for (c, d) in o.ranges):
                    self.ov.append(o)
        return self.ov


class Eng:
    def __init__(self, name, is_pe=False):
        self.name = name
        self.prog = []
        self.sem = None
        self.n = 0
        self.seen = {}
        self.is_pe = is_pe
        self.dsems = []
        self.dcnt = []
        self.dn = 0

    def wait(self, ev):
        if ev is None:
            return
        sem, val = ev
        if self.is_pe and sem is self.sem:
            return
        key = id(sem)
        if self.seen.get(key, 0) >= val:
            return
        self.seen[key] = val
        self.prog.append(("w", sem, val))

    def _deps(self, R, W):
        for b0 in R:
            for b in b0.group():
                self.wait(b.w)
            if b0.is_psum:
                for ev in list(b0.r.values()):
                    if ev[0] is not self.sem:
                        self.wait(ev)
        for b0 in W:
            for b in b0.group():
                self.wait(b.w)
                for ev in list(b.r.values()):
                    self.wait(ev)

    def _upd(self, R, W, ev):
        for b in R:
            b.r[id(ev[0])] = ev
        for b in W:
            b.w = ev
            b.r = {}

    def op(self, fn, R=(), W=()):
        self._deps(R, W)
        self.n += 1
        ev = (self.sem, self.n)
        self.prog.append(("i", [fn], self.sem, 1))
        self._upd(R, W, ev)

    def group(self, fns, R=(), W=()):
        self._deps(R, W)
        self.n += 1
        ev = (self.sem, self.n)
        self.prog.append(("i", list(fns), self.sem, 1))
        self._upd(R, W, ev)

    def dma(self, out, in_, R=(), W=(), **kw):
        i = self.dn % len(self.dsems)
        self.dn += 1
        sem = self.dsems[i]
        self.wait((sem, 16 * self.dcnt[i]))
        self._deps(R, W)
        self.dcnt[i] += 1
        ev = (sem, 16 * self.dcnt[i])
        self.prog.append(("i", [lambda e, o=out, a=in_, k=kw: e.dma_start(out=o, in_=a, **k)], sem, 16))
        self._upd(R, W, ev)
        return ev

    def replay(self, e):
        for it in self.prog:
            if it[0] == "w":
                e.wait_ge(it[1], it[2])
            else:
                ins = None
                for fn in it[1]:
                    ins = fn(e)
                ins.then_inc(it[2], it[3])


def build_program(nst=NST, do_sample=True, st_list=None):
    nc = bass.Bass("TRN2", target_bir_lowering=False)
    es = ExitStack()

    def din(name, shape):
        return nc.dram_tensor(name, list(shape), F32, kind="ExternalInput").ap()

    def dout(name, shape):
        return nc.dram_tensor(name, list(shape), F32, kind="ExternalOutput").ap()

    xp = din("xp", [S, D]); xs = din("xs", [64, D]); cpp = din("cc", [17, D])
    sconv = din("sconv", [16, 2, 256])
    kvin = [din("kv1", [16, 128, 512]), din("kv2", [16, 512, 512]), din("kv3", [16, 2048, 512])]
    w_ada = din("w_ada", [D, 6 * D]); b_adaT = din("b_adaT", [128, 48]); bga = din("bga", [17, 2048])
    n1gT = din("n1gT", [128, 8]); n2gT = din("n2gT", [128, 8])
    w_in = din("w_in", [D, 3072]); convwT = din("convwT", [128, 2, 3]); gncT = din("gncT", [128, 2]); gnaT = din("gnaT", [128, 6])
    w_out = din("w_out", [D, D]); w_gate = din("w_gate", [D, DFF]); w_up = din("w_up", [D, DFF]); w_down = din("w_down", [DFF, D])
    fgb = din("fgb", [128, D]); bt01 = din("bt01", [128, 8, 256]); bt2 = din("bt2", [128, 4, 384]); identd = din("ident", [128, 128])
    yp = dout("yp", [S, D]); ys = dout("ys", [64, D]); convp = dout("convp", [2, 256])
    kvp = [dout("kv1p", [128, 512]), dout("kv2p", [512, 512]), dout("kv3p", [2048, 512])]
    convs = dout("convs", [16, 2, 256])
    kvs = [dout("kv1s", [16, 128, 512]), dout("kv2s", [16, 512, 512]), dout("kv3s", [16, 2048, 512])]
    mod_scr = nc.dram_tensor("mod_scr", [17, 2048], F32, kind="Internal").ap()
    kv_scr = nc.dram_tensor("kv_scr", [64, 1536], F32, kind="Internal").ap()
    b_modscr = Buf("modscr"); b_kvscr = Buf("kvscr")
    wsc = {}
    b_wsc = {}
    for nm, shp in (("w_in", [D, 3072]), ("w_out", [D, D]), ("w_gate", [D, DFF]), ("w_up", [D, DFF]), ("w_down", [DFF, D])):
        wsc[nm] = nc.dram_tensor(nm + "_bf", shp, BF16, kind="Internal").ap()
        b_wsc[nm] = Buf(nm + "_bf")
    WSRC = {"w_in": w_in, "w_out": w_out, "w_gate": w_gate, "w_up": w_up, "w_down": w_down}

    arena = {"off": 16512}
    RNG = {}
    del ALLBUFS[:]

    def sb(name, shape, dt, at=None):
        nbytes = int(np.prod(shape[1:])) * (4 if dt == F32 else 2)
        nbytes = (nbytes + 31) // 32 * 32
        if at is None:
            at = arena["off"]
            arena["off"] += nbytes
            assert arena["off"] <= 229376, (name, arena["off"])
        RNG[name] = (at, at + nbytes)
        return nc.alloc_sbuf_tensor_at(name, list(shape), dt, offset=at)

    def region(nbytes):
        at = arena["off"]
        arena["off"] += nbytes
        assert arena["off"] <= 229376, arena["off"]
        return at

    K = 1024
    ident_f = sb("ident_f", [128, 128], F32); ident_b = sb("ident_b", [128, 128], BF16); ones_b = sb("ones_b", [128, 128], BF16)
    expb = [sb("expb01", [128, 8, 256], BF16), sb("expb2", [128, 4, 384], BF16)]
    bts = [sb("bts0", [128, 2, 2, 4, 2], F32), sb("bts1", [128, 4, 2, 2, 2], F32), sb("bts2", [128, 4, 2, 3, 2], F32)]
    Qbd = sb("Qbd", [128, 6, 64, 2], BF16)
    Es2 = [sb("Es2_0", [128, 64], BF16), sb("Es2_1", [128, 64], BF16)]
    Vn0b = sb("Vn0b", [4, 256], BF16); Vn12b = sb("Vn12b", [1, 4, 2, 256], BF16)
    vecs = sb("vecs", [128, 96], F32)
    modT = sb("modT", [128, 32, 17], F32)
    a1T = sb("a1T", [128, 8, 17], F32); a2T = sb("a2T", [128, 8, 17], F32)
    g1p = sb("g1p", [128, D], F32); g2p = sb("g2p", [128, D], F32); fg = sb("fg", [128, D], F32)
    small = sb("small", [128, 64], F32)
    convn = sb("convn", [128, 2, T], BF16)
    stage = [sb("stage0", [128, 512], F32), sb("stage1", [128, 512], F32)]
    xin = [sb("xin0", [128, D], F32), sb("xin1", [128, D], F32)]
    wbuf = [sb("wbuf0", [128, 8, 512], BF16), sb("wbuf1", [128, 8, 512], BF16)]
    hT = sb("hT", [128, 8, T], BF16)
    kv_at = region(60 * K)
    o = kv_at
    KT = []; QT = []; VP = []
    for g in range(3):
        KT.append(sb(f"KT{g}", [128, 2, SG[g], HIST[g] + CUR[g]], BF16, at=o)); o += 2 * SG[g] * (HIST[g] + CUR[g]) * 2
    for g in range(3):
        nb = (HIST[g] + CUR[g]) // 128
        VP.append(sb(f"VP{g}", [128, SG[g], nb, 256], BF16, at=o)); o += SG[g] * nb * 512
    QTall = sb("QTall", [128, 6, T], BF16, at=o)
    for g in range(3):
        QT.append(sb(f"QT{g}", [128, 2, SG[g], CUR[g]], BF16, at=o + g * 2 * T * 2))
    o += 6 * T * 2
    assert o <= kv_at + 60 * K, o - kv_at
    o = kv_at
    A1s = sb("A1s", [128, 8, 64], F32, at=o); o += 2048
    SH1s = sb("SH1s", [128, 8, 64], F32, at=o); o += 2048
    A2s = sb("A2s", [128, 8, 64], F32, at=o); o += 2048
    SH2s = sb("SH2s", [128, 8, 64], F32, at=o); o += 2048
    mixs = sb("mixs", [128, 6, 64], BF16, at=o); o += 768
    blkK = [sb("blkK0", [128, 13, 256], F32, at=o), sb("blkK1", [128, 13, 256], F32, at=o + 13 * 1024)]; o += 13 * 2048
    KTs2 = [sb("KTs0", [128, 13, 2, 128], BF16, at=o), None]; o += 13 * 512
    Vs2 = [sb("Vs0", [128, 13, 256], BF16, at=o), None]; o += 13 * 512
    QTs = sb("QTs", [128, 6, 64], BF16, at=o); o += 768
    KTn = sb("KTn", [128, 6, 64], BF16, at=o); o += 768
    g1s = sb("g1s", [128, D], F32, at=o); o += 4096
    g2s = sb("g2s", [128, D], F32, at=o); o += 4096
    assert o <= kv_at + 60 * K, o - kv_at
    r2 = region(32 * K)
    AT = sb("AT", [128, 6, T], F32, at=r2); Zb = sb("Zb", [128, 2, T], F32, at=r2 + 24 * K)
    x1 = sb("x1", [128, 8, D], F32, at=r2)
    r4 = region(36 * K)
    xnb = [sb("xnb0", [128, D], BF16, at=r4), sb("xnb1", [128, D], BF16, at=r4 + 2 * K)]
    junk = sb("junk", [128, D], BF16, at=r4 + 4 * K)
    GB = sb("GB", [128, 2, 512], F32, at=r4 + 6 * K); GC = sb("GC", [128, 2, 512], F32, at=r4 + 10 * K)
    U = sb("U", [128, 2, 520], F32, at=r4 + 14 * K)
    Zt = sb("Zt", [128, 2, 2, 512], F32, at=r4 + 19 * K)
    Ucar = sb("Ucar", [128, 2, 2], F32)
    Ebuf = [sb("E0", [128, 384], BF16, at=r4), sb("E1", [128, 384], BF16, at=r4 + K), sb("E2", [128, 384], BF16, at=r4 + 2 * K)]
    sq = sb("sq", [128, 6, 512], BF16, at=r4 + 27 * K)
    rsb = sb("rsb", [128, 512], F32, at=r4 + 33 * K)
    Es = sb("Es", [128, 64], BF16, at=r4 + 35 * K)
    Vn0 = sb("Vn0", [4, 256], BF16, at=r4 + 35 * K + 256)
    Vn12 = sb("Vn12", [1, 4, 2, 256], BF16, at=r4 + 6 * K)
    actT = sb("actT", [128, 6, T], BF16, at=r4); wd = sb("wd", [128, 6, D], BF16, at=r4 + 12 * K)
    sgt = [sb("sg0", [128, 512], F32, at=r4 + 24 * K), sb("sg1", [128, 512], F32, at=r4 + 26 * K)]
    KTs2[1] = sb("KTs1", [128, 13, 2, 128], BF16, at=r4 + 10 * K)
    Vs2[1] = sb("Vs1", [128, 13, 256], BF16, at=r4 + 10 * K + 13 * 512)
    pbank = [nc.alloc_psum_tensor(f"pb{i}", [128, 512], F32) for i in range(8)]
    pbuf = [Buf(f"pb{i}") for i in range(8)]
    for b_ in pbuf:
        b_.is_psum = True

    pe = Eng("pe", is_pe=True); act = Eng("act"); dve = Eng("dve"); pool = Eng("pool"); sp = Eng("sp")
    engs = [pe, act, dve, pool, sp]
    print("arena end", arena["off"], 229376 - arena["off"])
    for e in engs:
        e.sem = es.enter_context(nc.semaphore("s_" + e.name))
    for e in (sp, pool):
        for i in range(8):
            e.dsems.append(es.enter_context(nc.semaphore(f"d_{e.name}{i}")))
            e.dcnt.append(0)

    btf01 = sb("btf01", [128, 8, 256], F32, at=r2); btf2 = sb("btf2", [128, 4, 384], F32, at=r2 + 8 * K); bttmp = sb("bttmp", [128, 2048], F32, at=r2 + 16 * K)
    cin = sb("cin", [17, D], F32, at=r4); csil = sb("csil", [17, D], F32, at=r4 + 4 * K); siluT = sb("siluT", [128, 8, 17], BF16, at=r4 + 8 * K)
    gat = sb("gat", [17, 2048], F32, at=r4 + 10 * K); bgat = sb("bgat", [17, 2048], F32, at=r4 + 18 * K)
    GROUPS = {
        "const": ["ident_f", "ident_b", "ones_b", "expb01", "expb2", "bts0", "bts1", "bts2"],
        "vecs": ["vecs"], "modT": ["modT"], "aT": ["a1T", "a2T"], "gp": ["g1p", "g2p", "fg"], "small": ["small"], "convn": ["convn"],
        "hT": ["hT"], "KT": ["KT0", "KT1", "KT2"], "VP": ["VP0", "VP1", "VP2"], "QT": ["QTall"], "AT": ["AT"], "Zb": ["Zb"],
        "junk": ["junk"], "GB": ["GB"], "GC": ["GC"], "U": ["U"], "Zt": ["Zt"], "Ucar": ["Ucar"], "sq": ["sq"], "rsb": ["rsb"],
        "actT": ["actT"], "wd": ["wd"], "samp": ["A1s", "SH1s", "A2s", "SH2s", "g1s", "g2s"], "blk0": ["blkK0"], "blk1": ["blkK1"], "KTs0": ["KTs0"], "KTs1": ["KTs1"], "Vs0": ["Vs0"], "Vs1": ["Vs1"],
        "Es": ["Es"], "Qbd": ["Qbd"], "Es2_0": ["Es2_0"], "Es2_1": ["Es2_1"], "Vnb0": ["Vn0", "Vn12"], "Vnb1": ["Vn0b", "Vn12b"], "QTs": ["QTs", "KTn"], "mixs": ["mixs"],
        "btf": ["btf01", "btf2"], "bttmp": ["bttmp"], "cin": ["cin", "bgat"], "csil": ["csil"], "siluT": ["siluT"], "gat": ["gat"],
    }
    B = {n: Buf(n, [RNG[t] for t in ts_]) for n, ts_ in GROUPS.items()}
    b_xin = [Buf("xin0", [RNG["xin0"]]), Buf("xin1", [RNG["xin1"]])]; b_xnb = [Buf("xnb0", [RNG["xnb0"]]), Buf("xnb1", [RNG["xnb1"]])]
    b_w = [Buf("w0", [RNG["wbuf0"]]), Buf("w1", [RNG["wbuf1"]])]
    b_E = [Buf(f"E{i}", [RNG[f"E{i}"]]) for i in range(3)]; b_stage = [Buf(f"st{i}", [RNG[f"stage{i}"]]) for i in range(2)]
    b_sg = [Buf(f"sg{i}", [RNG[f"sg{i}"]]) for i in range(2)]
    b_x1t = [Buf(f"x1t{i}", [(r2 + i * 4096, r2 + (i + 1) * 4096)]) for i in range(8)]
    b_small = [Buf(f"small{i}") for i in range(8)]
    b_hT = [[Buf(f"hT{i}_{p}") for p in range(2)] for i in range(8)]
    HTR = [b for pr in b_hT for b in pr]
    cnt = {"w": 0, "xin": 0, "E": 0, "stage": 0, "pb": 0}
    outs_done = []

    def mm(out, lhsT, rhs, start, stop, **kw):
        return lambda e: e.matmul(out, lhsT=lhsT, rhs=rhs, start=start, stop=stop, **kw)

    def act_fn(out, in_, func, **kw):
        return lambda e: e.activation(out=out, in_=in_, func=func, **kw)

    def tt(out, in0, in1, op):
        return lambda e: e.tensor_tensor(out=out, in0=in0, in1=in1, op=op)

    def ts(out, in0, s1, s2, op0, op1=None):
        if op1 is None:
            return lambda e: e.tensor_scalar(out=out, in0=in0, scalar1=s1, scalar2=None, op0=op0)
        return lambda e: e.tensor_scalar(out=out, in0=in0, scalar1=s1, scalar2=s2, op0=op0, op1=op1)

    def stt(out, in0, scalar, in1, op0, op1):
        return lambda e: e.scalar_tensor_tensor(out=out, in0=in0, scalar=scalar, in1=in1, op0=op0, op1=op1)

    def cp(out, in_):
        return lambda e: e.tensor_copy(out=out, in_=in_)

    epsc = vecs[:, 78:79]

    def rstd_ops(dst, src, scale, R, W):
        act.op(act_fn(dst, src, AF.Ln, scale=scale, bias=vecs[0:dst.shape[0], 78:79]), R=R + [B["vecs"]], W=W)
        act.op(act_fn(dst, dst, AF.Exp, scale=-0.5), R=W, W=W)

    piece = {}
    pending_wb = []

    def pbuf_of(key):
        if key not in piece:
            piece[key] = Buf("piece%d" % len(piece))
        return piece[key]

    def flush_wb():
        while pending_wb:
            o_, i_, rb, wbf = pending_wb.pop(0)
            sp.dma(o_, i_, R=[rb], W=[wbf])

    def load_w(src_list):
        s = cnt["w"] % 2
        cnt["w"] += 1
        flush_wb()
        for item in src_list(s):
            d, a = item[0], item[1]
            rb = [item[2]] if (len(item) > 2 and item[2] is not None) else []
            pool.dma(d, a, R=rb, W=[b_w[s]])
            if len(item) > 3 and item[3] is not None:
                pending_wb.append(item[3])
        return s

    sp.dma(ident_f[:], identd, W=[B["const"]])
    sp.dma(vecs[:, 0:48], b_adaT, W=[B["vecs"]]); sp.dma(vecs[:, 48:56], n1gT, W=[B["vecs"]]); sp.dma(vecs[:, 56:64], n2gT, W=[B["vecs"]])
    sp.dma(vecs[:, 64:70], convwT.rearrange("p c k -> p (c k)"), W=[B["vecs"]]); sp.dma(vecs[:, 70:72], gncT, W=[B["vecs"]]); sp.dma(vecs[:, 72:78], gnaT, W=[B["vecs"]])
    sp.dma(fg[:], fgb, W=[B["gp"]])
    dve.op(lambda e: e.memset(vecs[:, 78:79], EPS), W=[B["vecs"]])
    dve.op(lambda e: e.memset(ones_b[:], 1.0), W=[B["const"]])
    dve.op(cp(ident_b[:], ident_f[:]), R=[B["const"]], W=[B["const"]])
    dve.op(lambda e: e.memset(Ucar[:], 0.0), W=[B["Ucar"]])
    for g in range(3):
        pool.op(lambda e, g=g: e.memset(KT[g][:], 0.0), W=[B["KT"]])
        pool.op(lambda e, g=g: e.memset(VP[g][:], 0.0), W=[B["VP"]])
    sp.dma(btf01[:], bt01, W=[B["btf"]]); sp.dma(btf2[:], bt2, W=[B["btf"]])
    for (f, eb) in ((btf01, expb[0]), (btf2, expb[1])):
        act.op(act_fn(eb[:].rearrange("p a b -> p (a b)"), f[:].rearrange("p a b -> p (a b)"), AF.Exp), R=[B["btf"]], W=[B["const"]])
    v01 = btf01[:].rearrange("p h (o q) -> p h o q", o=2); v2 = btf2[:].rearrange("p h (o q) -> p h o q", o=3)
    for hp in range(2):
        for cp_ in range(2):
            dve.op(cp(bts[0][:, cp_, :, :, hp], v01[:, 2 * cp_ + hp, :, 0:4]), R=[B["btf"]], W=[B["const"]])
        for rho in range(4):
            dve.op(cp(bts[1][:, rho, :, :, hp], v01[:, 4 + hp:8:2, :, 0]), R=[B["btf"]], W=[B["const"]])
            dve.op(cp(bts[2][:, rho, :, :, hp], v2[:, hp:4:2, :, 0]), R=[B["btf"]], W=[B["const"]])
    pending_copies = []
    for g in (2, 1, 0):
        L = LG[g]
        for b in range(16):
            for r0 in range(4, L, 64):
                r1 = min(L, r0 + 64)
                pending_copies.append((kvs[g][b, r0 - 4:r1 - 4, :], kvin[g][b, r0:r1, :]))

    def issue_copies(n):
        for _ in range(n):
            if pending_copies:
                o_, i_ = pending_copies.pop(0)
                outs_done.append(sp.dma(o_, i_))
    sp.dma(cin[:], cpp, W=[B["cin"]]); sp.dma(bgat[:], bga, W=[B["cin"]])
    act.op(act_fn(csil[:], cin[:], AF.Silu), R=[B["cin"]], W=[B["csil"]])
    pe.group([lambda e, c=c: e.transpose(out=pbank[0][:, c * 17:(c + 1) * 17], in_=csil[:, c * 128:(c + 1) * 128], identity=ident_f[0:17, 0:17]) for c in range(8)],
             R=[B["csil"], B["const"]], W=[pbuf[0]])
    dve.op(cp(siluT[:].rearrange("p c b -> p (c b)"), pbank[0][:, 0:136]), R=[pbuf[0]], W=[B["siluT"]])
    MODCH = list(range(0, 16)) + list(range(24, 40))
    for blk in range(12):
        s = load_w(lambda s, blk=blk: [(wbuf[s][:], w_ada[:, blk * 512:(blk + 1) * 512].rearrange("(c p) n -> p c n", p=128))])
        if blk in (4, 5, 10, 11):
            col = (blk - 4) * 512 if blk < 6 else 1024 + (blk - 10) * 512
            pb = 1
            pe.group([mm(pbank[pb][0:17, :], siluT[:, k, :], wbuf[s][:, k, :], k == 0, k == 7) for k in range(8)], R=[B["siluT"], b_w[s]], W=[pbuf[pb]])
            dve.op(tt(gat[:, col:col + 512], pbank[pb][0:17, :], bgat[:, col:col + 512], ALU.add), R=[pbuf[pb], B["cin"]], W=[B["gat"]])
        else:
            pb = 2 + (blk % 2)
            fns = []
            for cc in range(4):
                for k in range(8):
                    fns.append(mm(pbank[pb][:, cc * 17:(cc + 1) * 17], wbuf[s][:, k, cc * 128:(cc + 1) * 128], siluT[:, k, :], k == 0, k == 7))
            pe.group(fns, R=[B["siluT"], b_w[s]], W=[pbuf[pb]])
            for cc in range(4):
                ch = blk * 4 + cc
                mi = MODCH.index(ch)
                dve.op(ts(modT[:, mi, :], pbank[pb][:, cc * 17:(cc + 1) * 17], vecs[:, ch:ch + 1], None, ALU.add), R=[pbuf[pb], B["vecs"]], W=[B["modT"]])
    for c in range(8):
        dve.op(ts(a1T[:, c, :], modT[:, 8 + c, :], 1.0, vecs[:, 48 + c:49 + c], ALU.add, ALU.mult), R=[B["modT"], B["vecs"]], W=[B["aT"]])
        dve.op(ts(a2T[:, c, :], modT[:, 24 + c, :], 1.0, vecs[:, 56 + c:57 + c], ALU.add, ALU.mult), R=[B["modT"], B["vecs"]], W=[B["aT"]])
    sp.dma(mod_scr, gat[:], R=[B["gat"]], W=[b_modscr])
    sp.dma(g1p[:], mod_scr[0:1, 0:1024].partition_broadcast(128), R=[b_modscr], W=[B["gp"]])
    sp.dma(g2p[:], mod_scr[0:1, 1024:2048].partition_broadcast(128), R=[b_modscr], W=[B["gp"]])

    def norm_to_hT(src_tile, npart, ntiles, sample, aT, shoff, As, SHs):
        def stage1(t):
            xa, xb_ = src_tile(t)
            s = t % 2
            act.op(act_fn(junk[0:npart, :], xa, AF.Square, scale=1.0 / 32.0, accum_out=small[0:npart, t:t + 1]), R=[xb_], W=[B["junk"], b_small[t]])
            rstd_ops(small[0:npart, 16 + t:17 + t], small[0:npart, t:t + 1], 1.0, [b_small[t]], [b_small[t]])
            dve.op(ts(xnb[s][0:npart, :], xa, small[0:npart, 16 + t:17 + t], None, ALU.mult), R=[xb_, b_small[t]], W=[b_xnb[s]])

        def stage2(t):
            s = t % 2
            pb = 4 + (t % 2)
            tp = pbank[pb][:].bitcast(BF16)
            pe.group([lambda e, c=c, tp=tp, s=s: e.transpose(out=tp[:, c * 128:c * 128 + npart], in_=xnb[s][0:npart, c * 128:(c + 1) * 128], identity=ident_b[0:npart, 0:npart]) for c in range(8)],
                     R=[b_xnb[s], B["const"]], W=[pbuf[pb]])
            for c in range(8):
                src = tp[:, c * 128:c * 128 + npart]
                dst = hT[:, c, t * 128:t * 128 + npart]
                eng = dve if t % 2 == 0 else act
                if not sample:
                    if eng is dve:
                        dve.op(ts(dst, src, aT[:, c, 0:1], modT[:, shoff + c, 0:1], ALU.mult, ALU.add), R=[pbuf[pb], B["aT"], B["modT"]], W=[b_hT[t][0]])
                    else:
                        act.op(act_fn(dst, src, AF.Identity, scale=aT[:, c, 0:1], bias=modT[:, shoff + c, 0:1]), R=[pbuf[pb], B["aT"], B["modT"]], W=[b_hT[t][0]])
                else:
                    dve.op(tt(sgt[0][:, 0:npart], src, As[:, c, :], ALU.mult), R=[pbuf[pb], B["samp"]], W=[b_sg[0]])
                    dve.op(tt(dst, sgt[0][:, 0:npart], SHs[:, c, :], ALU.add), R=[b_sg[0], B["samp"]], W=[b_hT[t][0]])
        stage1(0)
        for t in range(ntiles):
            if t + 1 < ntiles:
                stage1(t + 1)
            stage2(t)

    use_scr = {"on": False}

    def wsrc(w, c0, n, s, dst0=0):
        scr = wsc[w][:, c0:c0 + n].rearrange("(c p) n -> p c n", p=128)
        dst = wbuf[s][:, :, dst0:dst0 + n]
        pb_ = pbuf_of((w, c0, n))
        if not use_scr["on"]:
            return (dst, WSRC[w][:, c0:c0 + n].rearrange("(c p) n -> p c n", p=128), None, (scr, dst, b_w[s], pb_))
        return (dst, scr, pb_, None)

    def conv_part1(hi, tok0, ntok, nb, tb, sample):
        for c in range(2):
            uv = U[:, c, 0:nb * (tb + 2)].rearrange("p (b t) -> p b t", b=nb)
            zv = Zt[:, hi, c, 0:ntok].rearrange("p (b t) -> p b t", b=nb)
            gbv = GB[:, c, 0:ntok].rearrange("p (b t) -> p b t", b=nb)
            w0 = vecs[:, 64 + 3 * c:65 + 3 * c]; w1 = vecs[:, 65 + 3 * c:66 + 3 * c]; w2 = vecs[:, 66 + 3 * c:67 + 3 * c]
            dve.op(ts(zv, uv[:, :, 0:tb], w0, None, ALU.mult), R=[B["U"], B["vecs"]], W=[B["Zt"]])
            dve.op(stt(zv, uv[:, :, 1:tb + 1], w1, zv, ALU.mult, ALU.add), R=[B["U"], B["Zt"], B["vecs"]], W=[B["Zt"]])
            dve.op(stt(zv, uv[:, :, 2:tb + 2], w2, zv, ALU.mult, ALU.add), R=[B["U"], B["Zt"], B["vecs"]], W=[B["Zt"]])
            dve.op(tt(zv, zv, gbv, ALU.mult), R=[B["GB"], B["Zt"]], W=[B["Zt"]])
            act.op(act_fn(sq[:, 2 * hi + c, 0:ntok], Zt[:, hi, c, 0:ntok], AF.Square), R=[B["Zt"]], W=[B["sq"]])

    def conv_part2(hi, tok0, ntok):
        pe.group([mm(pbank[6][:, 0:ntok], ones_b[:], sq[:, 2 * hi + c, 0:ntok], c == 0, c == 1) for c in range(2)], R=[B["sq"], B["const"]], W=[pbuf[6]])
        rstd_ops(rsb[:, 0:ntok], pbank[6][:, 0:ntok], 1.0 / 256.0, [pbuf[6]], [B["rsb"]])
        for c in range(2):
            dve.op(stt(convn[:, c, tok0:tok0 + ntok], Zt[:, hi, c, 0:ntok], vecs[:, 70 + c:71 + c], rsb[:, 0:ntok], ALU.mult, ALU.mult),
                   R=[B["Zt"], B["rsb"], B["vecs"]], W=[B["convn"]])

    def proj_in(st, halves, sample):
        nb, tb = (16, 4) if sample else (1, 512)
        sA = load_w(lambda s: [wsrc("w_in", 0, 512, s)])
        sB = load_w(lambda s: [wsrc("w_in", 512, 256, s)])
        for hi, (tok0, ntok) in enumerate(halves):
            for cc in range(4):
                pb = cc % 2
                pe.group([mm(pbank[pb][:, 0:ntok], wbuf[sA][:, k, cc * 128:(cc + 1) * 128], hT[:, k, tok0:tok0 + ntok], k == 0, k == 7) for k in range(8)],
                         R=HTR + [b_w[sA]], W=[pbuf[pb]])
                dstt, bb = (GB, B["GB"]) if cc < 2 else (GC, B["GC"])
                act.op(act_fn(dstt[:, cc % 2, 0:ntok], pbank[pb][:, 0:ntok], AF.Copy), R=[pbuf[pb]], W=[bb])
            for c in range(2):
                pb = c % 2
                pe.group([mm(pbank[pb][:, 0:ntok], wbuf[sB][:, k, c * 128:(c + 1) * 128], hT[:, k, tok0:tok0 + ntok], k == 0, k == 7) for k in range(8)],
                         R=HTR + [b_w[sB]], W=[pbuf[pb]])
                uv = U[:, c, 0:nb * (tb + 2)].rearrange("p (b t) -> p b t", b=nb)
                if not sample:
                    dve.op(cp(U[:, c, 0:2], Ucar[:, c, :]), R=[B["Ucar"]], W=[B["U"]])
                dve.op(tt(uv[:, :, 2:tb + 2], pbank[pb][:, 0:ntok].rearrange("p (b t) -> p b t", b=nb), GC[:, c, 0:ntok].rearrange("p (b t) -> p b t", b=nb), ALU.mult),
                       R=[pbuf[pb], B["GC"]], W=[B["U"]])
                if not sample:
                    dve.op(cp(Ucar[:, c, :], U[:, c, tb:tb + 2]), R=[B["U"]], W=[B["Ucar"]])
            conv_part1(hi, tok0, ntok, nb, tb, sample)
            if sample:
                for c in range(2):
                    uv = U[:, c, 0:96].rearrange("p (b t) -> p b t", b=16)
                    for t_ in range(2):
                        outs_done.append(sp.dma(convs[:, t_, c * 128:(c + 1) * 128].rearrange("b p -> p b"), uv[:, :, 4 + t_], R=[B["U"]], allow_slow_non_contiguous=True))
            elif st == NST - 1 and hi == 1:
                for c in range(2):
                    outs_done.append(sp.dma(convp[:, c * 128:(c + 1) * 128].rearrange("t p -> p t"), U[:, c, 512:514], R=[B["U"]], allow_slow_non_contiguous=True))
        for (c0, n, kind) in ((768, 512, "q01"), (1280, 256, "q2"), (1536, 512, "k01"), (2048, 256, "k2")):
            s = load_w(lambda s, c0=c0, n=n: [wsrc("w_in", c0, n, s)])
            for cc in range(n // 128):
                gi = (cc // 2) if kind[1:] == "01" else 2
                ch = cc % 2
                for (tok0, ntok) in halves:
                    pb = cnt["pb"] % 2
                    cnt["pb"] += 1
                    pe.group([mm(pbank[pb][:, 0:ntok], wbuf[s][:, k, cc * 128:(cc + 1) * 128], hT[:, k, tok0:tok0 + ntok], k == 0, k == 7) for k in range(8)],
                             R=HTR + [b_w[s]], W=[pbuf[pb]])
                    src = pbank[pb][:, 0:ntok]
                    eng = act if cnt["pb"] % 2 == 0 else dve
                    if sample:
                        dst = (QTs if kind[0] == "q" else KTn)[:, gi * 2 + ch, 0:ntok]
                        bb = B["QTs"]
                    else:
                        sg_ = SG[gi]
                        m0 = tok0 // sg_
                        mlen = ntok // sg_
                        if kind[0] == "q":
                            dst = QT[gi][:, ch, :, m0:m0 + mlen]; bb = B["QT"]
                        else:
                            dst = KT[gi][:, ch, :, HIST[gi] + m0:HIST[gi] + m0 + mlen]; bb = B["KT"]
                        src = src.rearrange("p (m r) -> p r m", r=sg_)
                    scale = 0.125 if kind[0] == "q" else 1.0
                    if eng is act:
                        act.op(act_fn(dst, src, AF.Copy, scale=scale), R=[pbuf[pb]], W=[bb])
                    else:
                        dve.op(ts(dst, src, scale, None, ALU.mult), R=[pbuf[pb]], W=[bb])
        for hi, (tok0, ntok) in enumerate(halves):
            conv_part2(hi, tok0, ntok)
        for g in range(3):
            s = load_w(lambda s, g=g: [wsrc("w_in", 1536 + 256 * g, 256, s, 0), wsrc("w_in", 2304 + 256 * g, 256, s, 256)])
            if sample:
                pb = 2 + g % 2
                pe.group([mm(pbank[pb][0:64, :], hT[:, k, 0:64], wbuf[s][:, k, :], k == 0, k == 7) for k in range(8)], R=HTR + [b_w[s]], W=[pbuf[pb]])
                ss_ = cnt["stage"] % 2; cnt["stage"] += 1
                dve.op(cp(stage[ss_][0:64, :], pbank[pb][0:64, :]), R=[pbuf[pb]], W=[b_stage[ss_]])
                sp.dma(kv_scr[:, g * 512:(g + 1) * 512], stage[ss_][0:64, :], R=[b_stage[ss_]], W=[b_kvscr])
                for b in range(16):
                    outs_done.append(sp.dma(kvs[g][b, LG[g] - 4:LG[g], :], stage[ss_][4 * b:4 * b + 4, :], R=[b_stage[ss_]]))
                continue
            sg_ = SG[g]
            nblk_cur = CUR[g] // 128
            hb = HIST[g] // 128
            for r in range(sg_):
                for mb in range(nblk_cur):
                    pos0 = sg_ * mb * 128 + r
                    gpos = st * T + pos0
                    need_k = (gpos + sg_ * 127) >= S - LG[g]
                    ncol = 512 if need_k else 256
                    c0 = 0 if need_k else 256
                    pb = 2 + cnt["pb"] % 2
                    cnt["pb"] += 1
                    pe.group([mm(pbank[pb][:, 0:ncol], hT[:, k, pos0:pos0 + sg_ * 127 + 1:sg_], wbuf[s][:, k, c0:c0 + ncol], k == 0, k == 7) for k in range(8)],
                             R=HTR + [b_w[s]], W=[pbuf[pb]])
                    if not need_k:
                        act.op(act_fn(VP[g][:, r, hb + mb, :], pbank[pb][:, 0:256], AF.Copy), R=[pbuf[pb]], W=[B["VP"]])
                    else:
                        ss_ = cnt["stage"] % 2; cnt["stage"] += 1
                        dve.op(cp(stage[ss_][:], pbank[pb][:, 0:512]), R=[pbuf[pb]], W=[b_stage[ss_]])
                        act.op(act_fn(VP[g][:, r, hb + mb, :], stage[ss_][:, 256:512], AF.Copy), R=[b_stage[ss_]], W=[B["VP"]])
                        row0 = gpos - (S - LG[g])
                        dst = kvp[g][row0:row0 + sg_ * 127 + 1:sg_, :]
                        outs_done.append(sp.dma(dst, stage[ss_][:], R=[b_stage[ss_]]))

    def attention_prompt(st):
        for g in range(3):
            sg_ = SG[g]; hb = HIST[g] // 128; nq_blocks = CUR[g] // 128
            ebt = expb[0] if g < 2 else expb[1]
            jobs = []
            for r in range(sg_):
                for mb in range(nq_blocks):
                    ne = 0
                    for o in range(NBLK[g]):
                        if st * nq_blocks + (mb - o) >= 0:
                            ne += 1
                    for p in range(2):
                        odi = 6 + (cnt["pb"] % 2)
                        cnt["pb"] += 1
                        for hp in range(2):
                            jobs.append(dict(r=r, mb=mb, ne=ne, p=p, hp=hp, odi=odi, si=4 + (cnt["E"] % 2), ei=cnt["E"] % 3))
                            cnt["E"] += 1

            def emit_S(j):
                r, mb, ne, p, hp = j["r"], j["mb"], j["ne"], j["p"], j["hp"]
                h = 2 * p + hp
                hh = (4 * g + h) if g < 2 else h
                Sps = pbank[j["si"]]; Sb = pbuf[j["si"]]
                n = ne * 128
                fns = []
                for o in range(ne):
                    kcol = (hb + mb - o) * 128
                    fns.append(mm(Sps[:, o * 128:(o + 1) * 128], KT[g][hp * 64:(hp + 1) * 64, p, r, kcol:kcol + 128],
                                  QT[g][hp * 64:(hp + 1) * 64, p, r, mb * 128:(mb + 1) * 128], True, True, skip_group_check=True))
                pe.group(fns, R=[B["KT"], B["QT"]], W=[Sb])
                act.op(act_fn(Ebuf[j["ei"]][:, 0:n], Sps[:, 0:n], AF.Exp), R=[Sb], W=[b_E[j["ei"]]])
                dve.op(tt(Ebuf[j["ei"]][:, 0:n], Ebuf[j["ei"]][:, 0:n], ebt[:, hh, 0:n], ALU.mult), R=[b_E[j["ei"]], B["const"]], W=[b_E[j["ei"]]])

            def emit_PV(j):
                r, mb, ne, p, hp = j["r"], j["mb"], j["ne"], j["p"], j["hp"]
                h = 2 * p + hp
                od = pbank[j["odi"]]; odb = pbuf[j["odi"]]
                E_ = Ebuf[j["ei"]]
                fns = []
                for o in range(ne):
                    fns.append(mm(od[hp * 64:(hp + 1) * 64, 0:128], VP[g][:, r, hb + mb - o, h * 64:(h + 1) * 64], E_[:, o * 128:(o + 1) * 128],
                                  o == 0, o == ne - 1, tile_position=(0, hp * 64), skip_group_check=True))
                for o in range(ne):
                    fns.append(mm(od[hp * 64:(hp + 1) * 64, 128:256], ones_b[:, 0:64], E_[:, o * 128:(o + 1) * 128],
                                  o == 0, o == ne - 1, tile_position=(0, hp * 64), skip_group_check=True))
                pe.group(fns, R=[B["VP"], b_E[j["ei"]], B["const"]], W=[odb])
                if hp == 1:
                    t0 = sg_ * mb * 128 + r
                    sl = slice(t0, t0 + sg_ * 127 + 1, sg_)
                    dve.op(cp(AT[:, 2 * g + p, sl], od[:, 0:128]), R=[odb], W=[B["AT"]])
                    if g == 0:
                        dve.op(cp(Zb[:, p, sl], od[:, 128:256]), R=[odb], W=[B["Zb"]])
                    else:
                        dve.op(tt(Zb[:, p, sl], od[:, 128:256], Zb[:, p, sl], ALU.add), R=[odb, B["Zb"]], W=[B["Zb"]])
            emit_S(jobs[0])
            if len(jobs) > 1:
                emit_S(jobs[1])
            for i, j in enumerate(jobs):
                if i + 2 < len(jobs):
                    emit_S(jobs[i + 2])
                emit_PV(j)
            if st < NST - 1:
                if g < 2:
                    pool.op(cp(KT[g][:, :, :, 0:HIST[g]], KT[g][:, :, :, CUR[g]:CUR[g] + HIST[g]]), R=[B["KT"]], W=[B["KT"]])
                    pool.op(cp(VP[g][:, :, 0:hb, :], VP[g][:, :, nq_blocks:nq_blocks + hb, :]), R=[B["VP"]], W=[B["VP"]])
                else:
                    for j in range(2):
                        pool.op(cp(KT[g][:, :, :, j * 128:(j + 1) * 128], KT[g][:, :, :, (j + 1) * 128:(j + 2) * 128]), R=[B["KT"]], W=[B["KT"]])
                        pool.op(cp(VP[g][:, :, j, :], VP[g][:, :, j + 1, :]), R=[B["VP"]], W=[B["VP"]])

    b_Zh = [Buf("Zh0"), Buf("Zh1")]; b_Ah = [Buf("Ah0"), Buf("Ah1")]; b_mix = [Buf("mix0"), Buf("mix1")]

    def combine(ntok_total, halves, mixdst, mixb):
        for h, (tok0, ntok) in enumerate(halves):
            sl = slice(tok0, tok0 + ntok)
            for p in range(2):
                act.op(act_fn(Zb[:, p, sl], Zb[:, p, sl], AF.Ln), R=[B["Zb"], b_Zh[h]], W=[b_Zh[h]])
                act.op(act_fn(Zb[:, p, sl], Zb[:, p, sl], AF.Exp, scale=-1.0), R=[B["Zb"], b_Zh[h]], W=[b_Zh[h]])
            for c in range(6):
                dve.op(tt(AT[:, c, sl], AT[:, c, sl], Zb[:, c % 2, sl], ALU.mult), R=[B["AT"], B["Zb"], b_Zh[h], b_Ah[h]], W=[b_Ah[h]])
        for h, (tok0, ntok) in enumerate(halves):
            sl = slice(tok0, tok0 + ntok)
            for c in range(6):
                act.op(act_fn(sq[:, c, 0:ntok], AT[:, c, sl], AF.Square), R=[B["AT"], b_Ah[h]], W=[B["sq"]])
            pe.group([mm(pbank[6][:, 0:ntok], ones_b[:], sq[:, c, 0:ntok], c == 0, c == 5) for c in range(6)], R=[B["sq"], B["const"]], W=[pbuf[6]])
            rstd_ops(rsb[:, 0:ntok], pbank[6][:, 0:ntok], 1.0 / 768.0, [pbuf[6]], [B["rsb"]])
            for c in range(6):
                dve.op(stt(mixdst[:, c, sl], AT[:, c, sl], vecs[:, 72 + c:73 + c], rsb[:, 0:ntok], ALU.mult, ALU.mult),
                       R=[B["AT"], b_Ah[h], B["rsb"], B["vecs"], mixb], W=[b_mix[h]])

    def post(ntiles, npart, xsrc, ydst, sample, mixsrc, ga1, ga2, aT, As, SHs):
        ntok = ntiles * 128 if not sample else npart
        halves = [(0, 512), (512, 512)] if not sample else [(0, 64)]
        s0 = load_w(lambda s: [wsrc("w_out", 0, 512, s)])
        s1 = load_w(lambda s: [wsrc("w_out", 512, 512, s)])
        def wout_tile(t):
            xi = cnt["xin"] % 2; cnt["xin"] += 1
            sp.dma(xin[xi][0:npart, :], xsrc(t), W=[b_xin[xi]])
            for hf, s in ((0, s0), (1, s1)):
                pb = hf
                fns = []
                for c in range(8):
                    lhs = convn[:, c, t * 128:t * 128 + npart] if c < 2 else mixsrc[:, c - 2, t * 128:t * 128 + npart]
                    fns.append(mm(pbank[pb][0:npart, :], lhs, wbuf[s][:, c, :], c == 0, c == 7))
                pe.group(fns, R=[B["convn"], b_mix[(t * 128) // 512 if not sample else 0], b_w[s]], W=[pbuf[pb]])
                xs_ = x1[0:npart, t, hf * 512:(hf + 1) * 512]
                dve.op(tt(xs_, pbank[pb][0:npart, :], ga1[0:npart, hf * 512:(hf + 1) * 512], ALU.mult), R=[pbuf[pb], B["gp"], B["samp"]], W=[b_x1t[t]])
                dve.op(tt(xs_, xs_, xin[xi][0:npart, hf * 512:(hf + 1) * 512], ALU.add), R=[b_xin[xi]], W=[b_x1t[t]])
            return x1[0:npart, t, :], b_x1t[t]
        norm_to_hT(wout_tile, npart, ntiles, sample, aT, 16, As, SHs)
        if not sample:
            prefetch_next_x()
        if pre_ffn_hook["fn"] is not None and not sample:
            pre_ffn_hook["fn"]()
            pre_ffn_hook["fn"] = None
        blks = [(pi, jb, j0 + 2 * jb) for pi, (j0, nj) in enumerate(FFPARTS) for jb in range(nj // 2)]

        def ld(i):
            issue_copies(8)
            ja = blks[i][2]
            return load_w(lambda s, ja=ja: [wsrc("w_gate", ja * 128, 256, s, 0), wsrc("w_up", ja * 128, 256, s, 256)])
        slots = {0: ld(0)}
        bi = 0
        for pi, (j0, nj) in enumerate(FFPARTS):
            for jb in range(nj // 2):
                if bi + 1 < len(blks):
                    slots[bi + 1] = ld(bi + 1)
                if jb == 0:
                    wd_scr = wsc["w_down"][j0 * 128:(j0 + nj) * 128, :].rearrange("(j p) n -> p j n", p=128)
                    wd_pb = pbuf_of(("w_down", j0))
                    if use_scr["on"]:
                        pool.dma(wd[:, 0:nj, :], wd_scr, R=[wd_pb], W=[B["wd"]])
                    else:
                        pool.dma(wd[:, 0:nj, :], w_down[j0 * 128:(j0 + nj) * 128, :].rearrange("(j p) n -> p j n", p=128), W=[B["wd"]])
                        pending_wb.append((wd_scr, wd[:, 0:nj, :], B["wd"], wd_pb))
                s = slots[bi]
                bi += 1
                for jj in range(2):
                    for (tok0, nt_) in halves:
                        pe.group([mm(pbank[0][:, 0:nt_], wbuf[s][:, k, jj * 128:(jj + 1) * 128], hT[:, k, tok0:tok0 + nt_], k == 0, k == 7) for k in range(8)],
                                 R=HTR + [b_w[s]], W=[pbuf[0]])
                        pe.group([mm(pbank[1][:, 0:nt_], wbuf[s][:, k, 256 + jj * 128:256 + (jj + 1) * 128], hT[:, k, tok0:tok0 + nt_], k == 0, k == 7) for k in range(8)],
                                 R=HTR + [b_w[s]], W=[pbuf[1]])
                        si = cnt["pb"] % 2; cnt["pb"] += 1
                        act.op(act_fn(sgt[si][:, 0:nt_], pbank[0][:, 0:nt_], AF.Silu), R=[pbuf[0]], W=[b_sg[si]])
                        dve.op(tt(actT[:, 2 * jb + jj, tok0:tok0 + nt_], sgt[si][:, 0:nt_], pbank[1][:, 0:nt_], ALU.mult), R=[b_sg[si], pbuf[1]], W=[B["actT"]])
            for t in range(ntiles):
                for hf in range(2):
                    pb = 2 + hf
                    pe.group([mm(pbank[pb][0:npart, :], actT[:, j, t * 128:t * 128 + npart], wd[:, j, hf * 512:(hf + 1) * 512], j == 0, j == nj - 1) for j in range(nj)],
                             R=[B["actT"], B["wd"]], W=[pbuf[pb]])
                    si = cnt["pb"] % 2; cnt["pb"] += 1
                    dve.op(tt(sgt[si][0:npart, :], pbank[pb][0:npart, :], ga2[0:npart, hf * 512:(hf + 1) * 512], ALU.mult), R=[pbuf[pb], B["gp"], B["samp"]], W=[b_sg[si]])
                    xs_ = x1[0:npart, t, hf * 512:(hf + 1) * 512]
                    dve.op(tt(xs_, xs_, sgt[si][0:npart, :], ALU.add), R=[b_sg[si]], W=[b_x1t[t]])
        for t in range(ntiles):
            xa = x1[0:npart, t, :]
            act.op(act_fn(junk[0:npart, :], xa, AF.Square, scale=1.0 / 32.0, accum_out=small[0:npart, 32 + t:33 + t]), R=[b_x1t[t]], W=[B["junk"], b_small[t]])
            rstd_ops(small[0:npart, 48 + t:49 + t], small[0:npart, 32 + t:33 + t], 1.0, [b_small[t]], [b_small[t]])
            dve.op(stt(xa, xa, small[0:npart, 48 + t:49 + t], fg[0:npart, :], ALU.mult, ALU.mult), R=[b_small[t], B["gp"]], W=[b_x1t[t]])
            outs_done.append(sp.dma(ydst(t), xa, R=[b_x1t[t]]))

    hoisted = {"done": False}
    pre_ffn_hook = {"fn": None}

    def sample_prep():
        hoisted["done"] = True
        for (dst, srcT, off) in ((A1s, a1T, None), (A2s, a2T, None), (SH1s, modT, 0), (SH2s, modT, 16)):
            for c in range(8):
                src = srcT[:, c, 1:17] if off is None else modT[:, off + c, 1:17]
                dve.op(cp(dst[:, c, :].rearrange("p (b t) -> p b t", t=4), src.unsqueeze(2).to_broadcast([128, 16, 4])), R=[B["aT"], B["modT"]], W=[B["samp"]])
        for b in range(16):
            sp.dma(g1s[4 * b:4 * b + 4, :], mod_scr[1 + b:2 + b, 0:1024].partition_broadcast(4), R=[b_modscr], W=[B["samp"]])
            sp.dma(g2s[4 * b:4 * b + 4, :], mod_scr[1 + b:2 + b, 1024:2048].partition_broadcast(4), R=[b_modscr], W=[B["samp"]])

    prefetched = {}
    cur_st = {"st": None}
    st_seq = list(st_list if st_list is not None else range(nst))

    def prefetch_next_x():
        st = cur_st["st"]
        if st is None or st not in st_seq:
            return
        i = st_seq.index(st)
        if i + 1 >= len(st_seq):
            return
        nst_ = st_seq[i + 1]
        for t in range(2):
            xi = cnt["xin"] % 2; cnt["xin"] += 1
            sp.dma(xin[xi][:], xp[nst_ * T + t * 128: nst_ * T + (t + 1) * 128, :], W=[b_xin[xi]])
            prefetched[(nst_, t)] = xi

    for st in (st_list if st_list is not None else range(nst)):
        def src_tile(t, st=st):
            if (st, t) in prefetched:
                xi = prefetched.pop((st, t))
                return xin[xi][:], b_xin[xi]
            xi = cnt["xin"] % 2; cnt["xin"] += 1
            sp.dma(xin[xi][:], xp[st * T + t * 128: st * T + (t + 1) * 128, :], W=[b_xin[xi]])
            return xin[xi][:], b_xin[xi]
        cur_st["st"] = st
        if do_sample and st == st_seq[-1]:
            pre_ffn_hook["fn"] = sample_prep
        norm_to_hT(src_tile, 128, 8, False, a1T, 0, None, None)
        proj_in(st, [(0, 512), (512, 512)], False)
        attention_prompt(st)
        combine(T, [(0, 512), (512, 512)], QTall, B["QT"])
        post(8, 128, lambda t, st=st: xp[st * T + t * 128: st * T + (t + 1) * 128, :], lambda t, st=st: yp[st * T + t * 128: st * T + (t + 1) * 128, :],
             False, QTall, g1p, g2p, a2T, None, None)
        flush_wb()
        use_scr["on"] = True

    def sample_pass():
        if not hoisted["done"]:
            sample_prep()
        for c in range(2):
            for t_ in range(2):
                sp.dma(U[:, c, 0:96].rearrange("p (b t) -> p b t", b=16)[:, :, t_], sconv[:, t_, c * 128:(c + 1) * 128].rearrange("b p -> p b"), W=[B["U"]], allow_slow_non_contiguous=True)

        def src_tile_s(t):
            sp.dma(xin[0][0:64, :], xs, W=[b_xin[0]])
            return xin[0][0:64, :], b_xin[0]
        norm_to_hT(src_tile_s, 64, 1, True, None, 0, A1s, SH1s)
        proj_in(0, [(0, 64)], True)
        dve.op(lambda e: e.memset(Qbd[:], 0.0), W=[B["Qbd"]])
        for i_ in (4, 5):
            dve.op(lambda e, i_=i_: e.memset(pbank[i_][:], 0.0), W=[pbuf[i_]])
        for hp in range(2):
            dve.op(cp(Qbd[hp * 64:(hp + 1) * 64, :, :, hp], QTs[hp * 64:(hp + 1) * 64, :, :]), R=[B["QTs"]], W=[B["Qbd"]])
        b_Es2 = [B["Es2_0"], B["Es2_1"]]
        NCOL = (32, 32, 48)
        NO = NBLK

        def scol(g, a, cp_, o):
            if g == 0:
                return cp_ * 16 + o * 8
            return a * (4 * NO[g]) + cp_ * (2 * NO[g]) + o * 2

        def oslot(c, b):
            bank, cc = (6, c) if c < 4 else (2, c - 4)
            return bank, cc * 128 + b * 8

        def dslot(g, b):
            bank, gg = (7, g) if g < 2 else (3, 0)
            return bank, gg * 256 + b * 16
        for b in range(16):
            par = b % 2
            Vn0_ = Vn0 if par == 0 else Vn0b
            Vn12_ = Vn12 if par == 0 else Vn12b
            vnb = B["Vnb%d" % par]
            pool.dma(Vn0_[:], kv_scr[4 * b:4 * b + 4, 256:512], R=[b_kvscr], W=[vnb])
            pool.dma(Vn12_[:, :, 0, :], kv_scr[4 * b:4 * b + 4, 768:1024].rearrange("(o n) c -> o n c", o=1), R=[b_kvscr], W=[vnb])
            pool.dma(Vn12_[:, :, 1, :], kv_scr[4 * b:4 * b + 4, 1280:1536].rearrange("(o n) c -> o n c", o=1), R=[b_kvscr], W=[vnb])
            bk = blkK[par]; bkb = B["blk%d" % par]
            KTs = KTs2[par]; Vs = Vs2[par]
            ktb = B["KTs%d" % par]; vsb = B["Vs%d" % par]
            sp.dma(bk[:, 0, :], kvin[0][b][:, 0:256], W=[bkb])
            sp.dma(bk[:, 1:5, :], kvin[1][b].rearrange("(i r) c -> i r c", r=4)[:, :, 0:256], W=[bkb])
            for k_ in range(2):
                sp.dma(bk[:, 5 + k_:13:2, :], kvin[2][b, k_ * 1024:(k_ + 1) * 1024, :].rearrange("(i r) c -> i r c", r=8)[:, 0:4, 0:256], W=[bkb])
            pool.dma(Vs[:, 0, :], kvin[0][b][:, 256:512], W=[vsb])
            pool.dma(Vs[:, 1:5, :], kvin[1][b].rearrange("(i r) c -> i r c", r=4)[:, :, 256:512], W=[vsb])
            for k_ in range(2):
                pool.dma(Vs[:, 5 + k_:13:2, :], kvin[2][b, k_ * 1024:(k_ + 1) * 1024, :].rearrange("(i r) c -> i r c", r=8)[:, 0:4, 256:512], W=[vsb])
            for q4 in range(7):
                pb = q4 % 2
                idx = [(blk, p) for blk in range(13) for p in range(2)][q4 * 4:(q4 + 1) * 4]
                pe.group([lambda e, j=j, blk=blk, p=p, pb=pb, bk=bk: e.transpose(out=pbank[pb][:, j * 128:(j + 1) * 128], in_=bk[:, blk, p * 128:(p + 1) * 128], identity=ident_f[:])
                          for j, (blk, p) in enumerate(idx)], R=[bkb, B["const"]], W=[pbuf[pb]])
                nn = len(idx)
                dve.op(cp(KTs[:].rearrange("p a b k -> p (a b k)")[:, q4 * 512:q4 * 512 + nn * 128], pbank[pb][:, 0:nn * 128]), R=[pbuf[pb]], W=[ktb])

            def blk_of(g, rho, o):
                if g == 0:
                    return 0
                return (1 + rho) if g == 1 else (5 + 2 * rho + (2 - o))
            for g in range(3):
                Sps = pbank[4 + (g % 2)]; Sb = pbuf[4 + (g % 2)]
                E_ = Es2[g % 2]; Eb = b_Es2[g % 2]
                nc_ = NCOL[g]
                fns = []
                for cp_ in range(2):
                    c = 2 * g + cp_
                    if g == 0:
                        q = Qbd[:, c, 4 * b:4 * b + 4, :]
                        fns.append(mm(Sps[0:4, scol(0, 0, cp_, 0):scol(0, 0, cp_, 0) + 8], KTn[:, c, 4 * b:4 * b + 4], q, True, True, skip_group_check=True))
                        fns.append(mm(Sps[:, scol(0, 0, cp_, 1):scol(0, 0, cp_, 1) + 8], KTs[:, 0, cp_, :], q, True, True, skip_group_check=True))
                    else:
                        for rho in range(4):
                            q = Qbd[:, c, 4 * b + rho, :]
                            c0 = scol(g, rho, cp_, 0)
                            fns.append(mm(Sps[0:1, c0:c0 + 2], KTn[:, c, 4 * b + rho:4 * b + rho + 1], q, True, True, skip_group_check=True))
                            for o in range(1, NO[g]):
                                c1 = scol(g, rho, cp_, o)
                                fns.append(mm(Sps[:, c1:c1 + 2], KTs[:, blk_of(g, rho, o), cp_, :], q, True, True, skip_group_check=True))
                pe.group(fns, R=[B["Qbd"], B["QTs"], ktb], W=[Sb])
                si = g % 2
                btf_ = bts[g][:].rearrange("p a b c d -> p (a b c d)")
                dve.op(tt(sgt[si][:, 0:nc_], Sps[:, 0:nc_], btf_, ALU.add), R=[Sb, B["const"]], W=[b_sg[si]])
                act.op(act_fn(E_[:, 0:nc_], sgt[si][:, 0:nc_], AF.Exp), R=[b_sg[si]], W=[Eb])
                fns = []
                wb = set()
                for cp_ in range(2):
                    c = 2 * g + cp_
                    obank, ocol = oslot(c, b)
                    wb.add(obank)
                    if g == 0:
                        dst = pbank[obank][:, ocol:ocol + 8]
                        c0 = scol(0, 0, cp_, 0); c1 = scol(0, 0, cp_, 1)
                        fns.append(mm(dst, Vn0_[0:4, cp_ * 128:(cp_ + 1) * 128], E_[0:4, c0:c0 + 8], True, False, skip_group_check=True))
                        fns.append(mm(dst, Vs[:, 0, cp_ * 128:(cp_ + 1) * 128], E_[:, c1:c1 + 8], False, True, skip_group_check=True))
                    else:
                        for rho in range(4):
                            dst = pbank[obank][:, ocol + 2 * rho:ocol + 2 * rho + 2]
                            c0 = scol(g, rho, cp_, 0)
                            fns.append(mm(dst, Vn12_[0:1, rho, g - 1, cp_ * 128:(cp_ + 1) * 128], E_[0:1, c0:c0 + 2], True, False, skip_group_check=True))
                            for o in range(1, NO[g]):
                                c1 = scol(g, rho, cp_, o)
                                fns.append(mm(dst, Vs[:, blk_of(g, rho, o), cp_ * 128:(cp_ + 1) * 128], E_[:, c1:c1 + 2], False, o == NO[g] - 1, skip_group_check=True))
                dbank, dcol = dslot(g, b)
                wb.add(dbank)
                dd = pbank[dbank][:, dcol:dcol + 16]
                if g == 0:
                    Ev = E_[:, 0:32].rearrange("p (c o k) -> p c o k", c=2, o=2)
                    fns.append(mm(dd, ones_b[0:4, :], Ev[0:4, :, 0, :], True, False, skip_group_check=True))
                    fns.append(mm(dd, ones_b[:, :], Ev[:, :, 1, :], False, True, skip_group_check=True))
                else:
                    Ev = E_[:, 0:nc_].rearrange("p (r c o h) -> p r c o h", r=4, c=2, o=NO[g])
                    fns.append(mm(dd, ones_b[0:1, :], Ev[0:1, :, :, 0, :], True, False, skip_group_check=True))
                    for o in range(1, NO[g]):
                        fns.append(mm(dd, ones_b[:, :], Ev[:, :, :, o, :], False, o == NO[g] - 1, skip_group_check=True))
                pe.group(fns, R=[Eb, vsb, vnb, B["const"]], W=[pbuf[i] for i in sorted(wb)])
        for hp in range(2):
            rows = slice(hp * 64, (hp + 1) * 64)
            dve.op(cp(AT[rows, 0:4, 0:64].rearrange("p c (b q) -> p c b q", q=4),
                      pbank[6][rows, :].rearrange("p (c b q h) -> p c b q h", c=4, b=16, q=4)[:, :, :, :, hp]), R=[pbuf[6]], W=[B["AT"]])
            dve.op(cp(AT[rows, 4:6, 0:64].rearrange("p c (b q) -> p c b q", q=4),
                      pbank[2][rows, 0:256].rearrange("p (c b q h) -> p c b q h", c=2, b=16, q=4)[:, :, :, :, hp]), R=[pbuf[2]], W=[B["AT"]])
            zv = Zb[rows, :, 0:64].rearrange("p c (b q) -> p c b q", q=4)
            dve.op(cp(zv, pbank[7][rows, 0:256].rearrange("p (b c q h) -> p c b q h", b=16, c=2, q=4)[:, :, :, :, hp]), R=[pbuf[7]], W=[B["Zb"]])
            dve.op(tt(zv, zv, pbank[7][rows, 256:512].rearrange("p (b r c h) -> p c b r h", b=16, r=4, c=2)[:, :, :, :, hp], ALU.add), R=[pbuf[7], B["Zb"]], W=[B["Zb"]])
            dve.op(tt(zv, zv, pbank[3][rows, 0:256].rearrange("p (b r c h) -> p c b r h", b=16, r=4, c=2)[:, :, :, :, hp], ALU.add), R=[pbuf[3], B["Zb"]], W=[B["Zb"]])
        combine(64, [(0, 64)], mixs, B["mixs"])
        post(1, 64, lambda t: xs, lambda t: ys, True, mixs, g1s, g2s, None, A2s, SH2s)


    issue_copies(10 ** 6)
    if do_sample:
        sample_pass()

    for ev in outs_done:
        sp.wait(ev)
    sp.prog.append(("w", sp.dsems[0], 16 * sp.dcnt[0]))
    tail_ev = sp.dma(kv_scr[0:17, 0:1024], mod_scr[:, 0:1024], R=[b_modscr], W=[b_kvscr])
    sp.wait(tail_ev)
    tail_ev = sp.dma(kv_scr[32:49, 0:1024], mod_scr[:, 0:1024], R=[b_modscr], W=[b_kvscr])
    sp.wait(tail_ev)

    print("engine counts", {e.name: (e.n, len(e.prog), e.dcnt) for e in engs})
    with nc.Block() as block:
        @block.tensor
        def _(e):
            pe.replay(e)

        @block.scalar
        def _(e):
            act.replay(e)

        @block.vector
        def _(e):
            dve.replay(e)

        @block.gpsimd
        def _(e):
            pool.replay(e)

        @block.sync
        def _(e):
            sp.replay(e)
    es.close()
    return nc


_CACHE = {}


def kernel(x_prompt, x_sample, state_conv, cache_kv1, cache_kv2, cache_kv3, c_prompt, c_sample,
           w_ada, b_ada, norm1_g, norm2_g, w_in, conv_w, gn_conv, gn_attn, w_out,
           w_gate, w_up, w_down, rel_bias, final_g):
    f = lambda a: np.ascontiguousarray(np.asarray(a, dtype=np.float32))
    x_prompt = f(x_prompt); x_sample = f(x_sample)
    if "nc" not in _CACHE:
        _CACHE["nc"] = build_program()
    nc = _CACHE["nc"]
    rb_aug = np.concatenate([f(rel_bias), np.full((1, 12), NEG, np.float32)], 0)
    idx = _bias_index_tiles()
    bt01 = np.stack([rb_aug[idx[h // 4], h] for h in range(8)], 1).reshape(128, 8, 256)
    bt2 = np.stack([rb_aug[idx[2], 8 + h] for h in range(4)], 1).reshape(128, 4, 384)
    b_ada_ = f(b_ada)[0]
    shared = {
        "w_ada": f(w_ada)[0], "b_adaT": f(b_ada_.reshape(48, 128).T),
        "bga": f(np.broadcast_to(np.concatenate([b_ada_[2048:3072], b_ada_[5120:6144]])[None, :], (17, 2048))),
        "n1gT": f(f(norm1_g)[0].reshape(8, 128).T), "n2gT": f(f(norm2_g)[0].reshape(8, 128).T),
        "w_in": f(w_in)[0], "convwT": f(f(conv_w)[0].reshape(3, 2, 128).transpose(2, 1, 0)),
        "gncT": f(f(gn_conv)[0].reshape(2, 128).T), "gnaT": f(f(gn_attn)[0].reshape(6, 128).T),
        "w_out": f(w_out)[0], "w_gate": f(w_gate)[0], "w_up": f(w_up)[0], "w_down": f(w_down)[0],
        "fgb": f(np.broadcast_to(f(final_g)[None, :], (128, D))), "bt01": f(bt01), "bt2": f(bt2),
        "ident": np.eye(128, dtype=np.float32),
    }
    in_maps = []
    for i in range(NCORES):
        bs = slice(16 * i, 16 * i + 16)
        m = dict(shared)
        m["xp"] = x_prompt[i]
        m["xs"] = x_sample[bs].reshape(64, D)
        m["cc"] = f(np.concatenate([f(c_prompt)[i:i + 1], f(c_sample)[bs]], 0))
        m["sconv"] = f(state_conv)[0, bs]
        m["kv1"] = f(cache_kv1)[0, bs].reshape(16, 128, 512)
        m["kv2"] = f(cache_kv2)[0, bs].reshape(16, 512, 512)
        m["kv3"] = f(cache_kv3)[0, bs].reshape(16, 2048, 512)
        in_maps.append(m)
    res = run_bass_kernel_spmd(nc, in_maps, core_ids=list(range(NCORES)))
    R = res.results
    cat = lambda k: np.stack([np.asarray(r[k], dtype=np.float32) for r in R], 0)
    y_prompt = cat("yp").reshape(8, S, D)
    y_sample = cat("ys").reshape(128, 4, D)
    conv_p = cat("convp").reshape(1, 8, 2, 256)
    kv1p = cat("kv1p").reshape(1, 8, 128, 2, 4, 64)
    kv2p = cat("kv2p").reshape(1, 8, 512, 2, 4, 64)
    kv3p = cat("kv3p").reshape(1, 8, 2048, 2, 4, 64)
    conv_s = cat("convs").reshape(1, 128, 2, 256)
    kv1s = cat("kv1s").reshape(1, 128, 128, 2, 4, 64)
    kv2s = cat("kv2s").reshape(1, 128, 512, 2, 4, 64)
    kv3s = cat("kv3s").reshape(1, 128, 2048, 2, 4, 64)
    return (y_prompt, y_sample, conv_p, kv1p, kv2p, kv3p, conv_s, kv1s, kv2s, kv3s)
```
